# Optimizing a Trainium2 kernel written in Bass

```python
import jax, jax.numpy as jnp
from jax import lax
import numpy as np

D_MODEL = 2048
BATCH = 4
SEQ = 2048
DEPTH = 4
DEC_BATCH = 128
DEC_SEQ = 1
PAST_LEN = 16384
PAGE_SIZE = 128

N_MIXERS = 2
N_A_LAYERS = (DEPTH + 1) // 2
N_B_LAYERS = DEPTH // 2
CHUNK = 128
D_A = D_MODEL
A_GROUP_DIM = 128
A_GROUPS = D_A // A_GROUP_DIM
D_B = D_MODEL
CONV_W = 3
D_FF = 4 * D_MODEL
RMS_EPS = 1e-6
LN_EPS = 1e-5

kernel_name = "hybrid_chunkmlp_shortconv_decoder_step"


def rms_norm(x, g):
    xf = x.astype(jnp.float32)
    y = xf * lax.rsqrt(jnp.mean(xf * xf, axis=-1, keepdims=True) + RMS_EPS)
    return (y * g.astype(jnp.float32)).astype(x.dtype)


def layer_norm(x, g, b):
    xf = x.astype(jnp.float32)
    xc = xf - jnp.mean(xf, axis=-1, keepdims=True)
    y = xc * lax.rsqrt(jnp.mean(xc * xc, axis=-1, keepdims=True) + LN_EPS)
    return (y * g.astype(jnp.float32) + b.astype(jnp.float32)).astype(x.dtype)


def chunk_spatial_mix(v, w_s, b_s):
    bsz, seq_len, _ = v.shape
    if seq_len < CHUNK:
        blk, pad = seq_len, 0
    else:
        blk, pad = CHUNK, (-seq_len) % CHUNK
    w = w_s[:, :blk, :blk]
    causal = jnp.tril(jnp.ones((blk, blk), dtype=bool))
    w = jnp.where(causal[None], w, jnp.zeros_like(w))
    vp = jnp.pad(v, ((0, 0), (0, pad), (0, 0)))
    n_chunks = (seq_len + pad) // blk
    vc = vp.reshape(bsz, n_chunks, blk, A_GROUPS, A_GROUP_DIM)
    bias = jnp.transpose(b_s[:, :blk])[None, None, :, :, None]
    f = jnp.einsum('gts,bcsgd->bctgd', w, vc) + bias
    return f.reshape(bsz, n_chunks * blk, D_A)[:, :seq_len]


def chunk_mlp_mixer(h, w_in, ln_g, ln_b, w_s, b_s, w_out):
    z = jax.nn.gelu(h @ w_in, approximate=False)
    u, v = jnp.split(z, 2, axis=-1)
    v = layer_norm(v, ln_g, ln_b)
    y = u * chunk_spatial_mix(v, w_s, b_s)
    seq_len = h.shape[1]
    tail = seq_len - CHUNK * ((seq_len - 1) // CHUNK)
    return y @ w_out, v[:, seq_len - tail:]


def short_conv_mixer(h, prefix, w_in, conv_w, w_out):
    gate_b, gate_c, hx = jnp.split(h @ w_in, 3, axis=-1)
    z = gate_c * hx
    zp = jnp.concatenate([prefix.astype(z.dtype), z], axis=1)
    seq_len = h.shape[1]
    conv = conv_w[0] * zp[:, 0:seq_len]
    for k in range(1, CONV_W):
        conv = conv + conv_w[k] * zp[:, k:k + seq_len]
    return (gate_b * conv) @ w_out, zp[:, seq_len:]


def trunk(x, conv_prefix, norm_mix_g, norm_ffn_g, a_w_in, a_ln_g, a_ln_b, a_w_s, a_b_s, a_w_out,
          b_w_in, b_conv_w, b_w_out, ffn_w_up, ffn_w_down, final_norm_g):
    v_tails, conv_tails = [], []
    for i in range(DEPTH):
        j = i // N_MIXERS
        h = rms_norm(x, norm_mix_g[i])
        if i % N_MIXERS == 0:
            out, vt = chunk_mlp_mixer(h, a_w_in[j], a_ln_g[j], a_ln_b[j], a_w_s[j], a_b_s[j], a_w_out[j])
            v_tails.append(vt)
        else:
            out, ct = short_conv_mixer(h, conv_prefix[j], b_w_in[j], b_conv_w[j], b_w_out[j])
            conv_tails.append(ct)
        x = x + out
        h = rms_norm(x, norm_ffn_g[i])
        x = x + jnp.square(jax.nn.relu(h @ ffn_w_up[i])) @ ffn_w_down[i]
    return rms_norm(x, final_norm_g), jnp.stack(v_tails), jnp.stack(conv_tails)


def setup_inputs(seed: int = 0) -> dict:
    key = jax.random.key(seed)
    ks = jax.random.split(key, 20)
    f32 = jnp.float32
    nrm = lambda k, s: jax.random.normal(k, s, dtype=f32)
    return {
        "x_prompt": nrm(ks[0], (BATCH, SEQ, D_MODEL)),
        "x_sample": nrm(ks[1], (DEC_BATCH, DEC_SEQ, D_MODEL)),
        "state_conv": nrm(ks[2], (N_B_LAYERS, DEC_BATCH, CONV_W - 1, D_B)),
        "norm_mix_g": 1.0 + 0.05 * nrm(ks[3], (DEPTH, D_MODEL)),
        "norm_ffn_g": 1.0 + 0.05 * nrm(ks[4], (DEPTH, D_MODEL)),
        "a_w_in": nrm(ks[5], (N_A_LAYERS, D_MODEL, 2 * D_A)) * D_MODEL ** -0.5,
        "a_ln_g": 1.0 + 0.05 * nrm(ks[6], (N_A_LAYERS, D_A)),
        "a_ln_b": 0.05 * nrm(ks[7], (N_A_LAYERS, D_A)),
        "a_w_s": nrm(ks[8], (N_A_LAYERS, A_GROUPS, CHUNK, CHUNK)) * CHUNK ** -0.5,
        "a_b_s": 1.0 + 0.1 * nrm(ks[9], (N_A_LAYERS, A_GROUPS, CHUNK)),
        "a_w_out": nrm(ks[10], (N_A_LAYERS, D_A, D_MODEL)) * D_A ** -0.5,
        "b_w_in": nrm(ks[11], (N_B_LAYERS, D_MODEL, 3 * D_B)) * D_MODEL ** -0.5,
        "b_conv_w": nrm(ks[12], (N_B_LAYERS, CONV_W, D_B)) * CONV_W ** -0.5,
        "b_w_out": nrm(ks[13], (N_B_LAYERS, D_B, D_MODEL)) * D_B ** -0.5,
        "ffn_w_up": nrm(ks[14], (DEPTH, D_MODEL, D_FF)) * D_MODEL ** -0.5,
        "ffn_w_down": nrm(ks[15], (DEPTH, D_FF, D_MODEL)) * D_FF ** -0.5,
        "final_norm_g": 1.0 + 0.05 * nrm(ks[16], (D_MODEL,)),
    }


def reference(x_prompt, x_sample, state_conv, norm_mix_g, norm_ffn_g, a_w_in, a_ln_g, a_ln_b, a_w_s,
              a_b_s, a_w_out, b_w_in, b_conv_w, b_w_out, ffn_w_up, ffn_w_down, final_norm_g):
    weights = (norm_mix_g, norm_ffn_g, a_w_in, a_ln_g, a_ln_b, a_w_s, a_b_s, a_w_out,
               b_w_in, b_conv_w, b_w_out, ffn_w_up, ffn_w_down, final_norm_g)
    prompt_prefix = jnp.zeros((N_B_LAYERS, x_prompt.shape[0], CONV_W - 1, D_B), dtype=x_prompt.dtype)
    y_prompt, v_prompt, conv_prompt = trunk(x_prompt, prompt_prefix, *weights)
    y_sample, v_sample, conv_sample = trunk(x_sample, state_conv, *weights)
    return (y_prompt, y_sample, v_prompt, v_sample, conv_prompt, conv_sample)
```

```python
import numpy as np
from contextlib import ExitStack
import concourse.bass as bass
import concourse.mybir as mybir
from concourse.bass_utils import run_bass_kernel_spmd

F32 = mybir.dt.float32
BF16 = mybir.dt.bfloat16
ALU = mybir.AluOpType
AF = mybir.ActivationFunctionType

D = 2048
KC = 16
NCH = 9
TP = NCH * 128
NSMP = 16
T = TP + NSMP
TILES = [(0, 512), (512, 1024), (1024, T)]
A_PASSES = [[(0, 384), (384, 640)], [(640, 1024), (1024, T)]]
NVS = 5
DEPTH = 4
DFF = 8192
NSLOT = 6
RMS_EPS = 1e-6
LN_EPS = 1e-5


def _wall_index():
    idx = {}
    n = 0
    for j in range(2):
        for b in range(16):
            idx[("a_in", j, b)] = n; n += 1
        for r in range(8):
            idx[("a_out", j, r)] = n; n += 1
        for b in range(24):
            idx[("b_in", j, b)] = n; n += 1
        for r in range(8):
            idx[("b_out", j, r)] = n; n += 1
    for i in range(DEPTH):
        for b in range(32):
            idx[("up", i, b)] = n; n += 1
        for r in range(32):
            idx[("down", i, r)] = n; n += 1
    return idx, n


WALL_IDX, NWALL = _wall_index()


def _load_sequence():
    seq = []
    for i in range(DEPTH):
        j = i // 2
        if i % 2 == 0:
            for _t in range(len(A_PASSES)):
                for n in range(8):
                    seq.append(("a_in", j, 8 + n))
                for gb in range(4):
                    seq.append(("a_in", j, 2 * gb))
                    seq.append(("a_in", j, 2 * gb + 1))
                    seq.append(("a_out", j, 2 * gb))
                    seq.append(("a_out", j, 2 * gb + 1))
        else:
            for bb in range(8):
                seq.append(("b_in", j, bb))
                seq.append(("b_in", j, 8 + bb))
                seq.append(("b_in", j, 16 + bb))
                seq.append(("b_out", j, bb))
        for b in range(16):
            seq.append(("up", i, 2 * b))
            seq.append(("up", i, 2 * b + 1))
            seq.append(("down", i, 2 * b))
            seq.append(("down", i, 2 * b + 1))
    return seq


class Res:
    __slots__ = ("name", "w", "r", "dsem", "dcnt")

    def __init__(self, name):
        self.name = name
        self.w = None
        self.r = {}
        self.dsem = None
        self.dcnt = 0


class Eng:
    def __init__(self, name, sem, is_pe=False):
        self.name = name
        self.sem = sem
        self.cnt = 0
        self.ops = []
        self.known = {}
        self.is_pe = is_pe


class Sched:
    def __init__(self, nc, stack):
        self.nc = nc
        self.stack = stack
        mk = lambda n: stack.enter_context(nc.semaphore(n))
        self.pe = Eng("pe", mk("s_pe"), True)
        self.act = Eng("act", mk("s_act"))
        self.dve = Eng("dve", mk("s_dve"))
        self.pool = Eng("pool", mk("s_pool"))
        self.sp = Eng("sp", mk("s_sp"))
        self.nsem = 0
        self.all_dma_res = []
        self.pending = []
        self.in_pending = False
        self.guarded = set()

    def _waits(self, E, reads, writes):
        need = {}

        def add(ev, same_ok):
            if ev is None:
                return
            sem, val, owner = ev
            if owner is E and same_ok:
                return
            k = id(sem)
            if k not in need or need[k][1] < val:
                need[k] = (sem, val)

        strict = (E.name == "pool")
        for R in reads:
            add(R.w, E.is_pe)
        for R in writes:
            add(R.w, not strict)
            for ev in R.r.values():
                add(ev, not strict)
        for k, (sem, val) in need.items():
            if E.known.get(k, 0) >= val:
                continue
            E.known[k] = val
            E.ops.append(lambda e, sem=sem, val=val: e.wait_ge(sem, val))

    def _guard(self, reads, writes):
        if self.pending and not self.in_pending:
            for R in list(reads) + list(writes):
                if R in self.guarded:
                    self.flush()
                    return

    def flush(self):
        while self.pending:
            self._run_pending()

    def _run_pending(self):
        f = self.pending.pop(0)
        self.in_pending = True
        try:
            f()
        finally:
            self.in_pending = False

    def op(self, E, fn, reads=(), writes=()):
        self._guard(reads, writes)
        self._waits(E, reads, writes)
        E.cnt += 1
        sem = E.sem
        E.ops.append(lambda e, fn=fn, sem=sem: fn(e).then_inc(sem, 1))
        ev = (sem, E.cnt, E)
        for R in reads:
            R.r[E.name] = ev
        for R in writes:
            R.w = ev
            R.r = {}

    def group(self, E, fns, reads=(), writes=()):
        self._guard(reads, writes)
        self._waits(E, reads, writes)
        E.cnt += 1
        sem = E.sem
        for fn in fns[:-1]:
            E.ops.append(lambda e, fn=fn: fn(e))
        last = fns[-1]
        E.ops.append(lambda e, fn=last, sem=sem: fn(e).then_inc(sem, 1))
        ev = (sem, E.cnt, E)
        for R in reads:
            R.r[E.name] = ev
        for R in writes:
            R.w = ev
            R.r = {}
        if E.is_pe and self.pending and not self.in_pending:
            self._run_pending()

    def dma(self, Q, fn, reads=(), writes=(), owner=None):
        if owner is None:
            owner = writes[0] if writes else reads[0]
        self._guard(reads, writes)
        if owner.dsem is None:
            owner.dsem = self.stack.enter_context(self.nc.semaphore("d%d" % self.nsem))
            self.nsem += 1
            self.all_dma_res.append(owner)
        self._waits(Q, reads, writes)
        owner.dcnt += 16
        sem = owner.dsem
        Q.ops.append(lambda e, fn=fn, sem=sem: fn(e).then_inc(sem, 16))
        ev = (sem, owner.dcnt, None)
        for R in reads:
            R.r["dma%d" % id(sem)] = ev
        for R in writes:
            R.w = ev
            R.r = {}


def build_program(n_layers=DEPTH, debug=False):
    nc = bass.Bass("TRN2", target_bir_lowering=False)
    dram_in = lambda n, s: nc.dram_tensor(n, s, F32, kind="ExternalInput").ap()
    dram_out = lambda n, s: nc.dram_tensor(n, s, F32, kind="ExternalOutput").ap()

    xin = dram_in("xin", [128, KC, T])
    wall = dram_in("wall", [NWALL, 128, 4096])
    gains_d = dram_in("gains", [128, 9, KC])
    convw_d = dram_in("convw", [128, 2, 3, KC])
    state_d = dram_in("state", [128, 2, KC, 2, NSMP])
    lngb_d = dram_in("lngb", [2, 8, 2, 256])
    aws_d = dram_in("aws", [2, 16, 128, 128])
    abs_d = dram_in("abs", [2, D])
    w00_d = dram_in("w00", [2, 16])
    b0_d = dram_in("b0", [2, 16])
    lnfm_d = dram_in("lnfm", [128, 2, 2, KC])
    lnb_d = dram_in("lnb", [2, D])

    y_o = dram_out("y_fm", [128, KC, T])
    v_o = dram_out("v_out", [2, 128, D])
    vs_o = dram_out("vs_out", [2, NSMP, D])
    cp_o = dram_out("convp", [128, 2, KC, 2])
    cs_o = dram_out("convs", [128, 2, KC, 2, NSMP])
    if debug:
        dbg_h = dram_out("dbg_h", [128, KC, T])
        dbg_v = dram_out("dbg_v", [128, 4, 2048])
        dbg_x = dram_out("dbg_x", [128, KC, T])

    with ExitStack() as stack:
        sb = lambda n, s, d: stack.enter_context(nc.sbuf_tensor(n, s, d))
        S = Sched(nc, stack)
        PE, ACT, DVE, POOL, SP = S.pe, S.act, S.dve, S.pool, S.sp

        x = sb("x", [128, KC, T], F32)
        h = sb("h", [128, KC, T], BF16)
        ring = sb("ring", [128, NSLOT, 4096], BF16)
        scr = sb("scr", [128, NVS, 2048], BF16)
        scr_f = scr.bitcast(F32)
        actb = sb("actb", [128, 2, 4, 512], BF16)
        actb_f = actb.bitcast(F32).rearrange("p a b c -> p (a b c)")
        tmpf = sb("tmpf", [128, 3, 512], F32)
        priv = sb("priv", [128, 3408], F32)
        privb = priv.bitcast(BF16)
        lngb = priv[:, 0:1024].rearrange("p (u a b) -> p u a b", u=2, a=2)
        wmt = privb[:, 2048:4096].rearrange("p (g s) -> p g s", g=16)
        qbuf = privb[:, 4096:6144].rearrange("p (g s) -> p g s", g=16)
        dg = privb[:, 6144:6400].rearrange("p (g s) -> p g s", g=16)
        w00 = priv[:, 3200:3216]
        b0b = priv[:, 3216:3232]
        qs = priv[:, 3232:3248]
        stats = priv[:, 3248:3368].rearrange("p (c a b) -> p c a b", c=5, a=4)
        mv = priv[:, 3368:3408].rearrange("p (a b) -> p a b", a=5)
        state = priv[:, 0:512].rearrange("p (a b c) -> p a b c", a=KC, b=2)
        csz = priv[:, 512:768].rearrange("p (a b) -> p a b", a=KC)
        zs = priv[:, 768:800].rearrange("p (a b) -> p a b", a=2)
        cpst = priv[:, 800:832].rearrange("p (a b) -> p a b", a=KC)
        gains = sb("gains_s", [128, 9, KC], F32)
        lnfm = sb("lnfm_s", [128, 2, 2, KC], F32)
        convw = sb("convw_s", [128, 2, 3, KC], F32)
        onesm = sb("onesm", [128, 128], BF16)
        onesr0 = sb("onesr0", [128, 128], BF16)
        ident = sb("ident", [128, 128], F32)
        dummy = sb("dummyt", [128, 8], F32)
        ps = stack.enter_context(nc.psum_tensor("ps", [128, 8, 512], F32))

        Rx = [Res("x%d" % t) for t in range(3)]
        Rh = [Res("h%d" % t) for t in range(3)]
        Rslot = [Res("slot%d" % s) for s in range(NSLOT)]
        Rscr = [Res("scr%d" % s) for s in range(NVS)]
        S.guarded = {Rx[1], Rx[2], Rh[1], Rh[2]} | set(Rscr)
        Ract = [Res("act0"), Res("act1")]
        Rtmpf = [Res("tmpf%d" % i) for i in range(3)]
        Rlngb = [Res("lngb0"), Res("lngb1")]
        Rwmt = Res("wmt")
        Rconst = Res("const")
        Rgains = Res("gains")
        Rconvw = Res("convw")
        Rstate = Res("state")
        Rcp = Res("cpst")
        Rcsz = Res("csz")
        Rzs = Res("zs")
        Rq = Res("qbuf")
        Rqs = Res("qs")
        Rb0 = Res("b0")
        Rlnfm = Res("lnfm")
        Rw00 = Res("w00")
        Rdg = Res("dg")
        Rstats = [Res("stats%d" % c) for c in range(5)]
        Rmv = [Res("mv%d" % c) for c in range(5)]
        Rdummy = Res("dummy")
        Rps = [Res("ps%d" % b) for b in range(8)]
        priv_all = Rlngb + [Rstate, Rcp, Rcsz, Rzs]

        rot = {"up": 0, "dn": 0, "tmpf": 0}

        def nxt(kind, n, base=0):
            v = rot[kind]
            rot[kind] = (v + 1) % n
            return base + v

        up_bank = lambda: nxt("up", 4, 0)
        dn_bank = lambda: nxt("dn", 4, 4)
        tmp_slot = lambda: nxt("tmpf", 3)

        def priv_fence():
            S.op(POOL, lambda e: e.memset(dummy[:], 0.0), writes=priv_all + [Rdummy])

        loads = _load_sequence()
        ws = {"next": 0, "released": [False] * len(loads), "cursor": 0}

        def ws_pump():
            while ws["next"] < len(loads):
                i = ws["next"]
                if i >= NSLOT and not ws["released"][i - NSLOT]:
                    break
                s = i % NSLOT
                widx = WALL_IDX[loads[i]]
                S.dma(POOL, lambda e, s=s, widx=widx: e.dma_start(out=ring[:, s, :], in_=wall[widx]),
                      writes=[Rslot[s]], owner=Rslot[s])
                ws["next"] += 1

        def ws_take(key):
            i = ws["cursor"]
            assert loads[i] == key, (loads[i], key)
            assert i < ws["next"], "weight load %d not issued yet" % i
            ws["cursor"] += 1
            return i

        def ws_release(i):
            ws["released"][i] = True
            ws_pump()

        S.dma(SP, lambda e: e.dma_start(out=gains[:], in_=gains_d), writes=[Rgains])
        for t, (c0, c1) in enumerate(TILES):
            S.dma(SP, lambda e, c0=c0, c1=c1: e.dma_start(out=x[:, :, c0:c1], in_=xin[:, :, c0:c1]),
                  writes=[Rx[t]])
        S.dma(SP, lambda e: e.dma_start(out=convw[:], in_=convw_d), writes=[Rconvw])
        S.dma(SP, lambda e: e.dma_start(out=lnfm[:], in_=lnfm_d), writes=[Rlnfm])
        S.op(POOL, lambda e: e.memset(onesm[:], 1.0 / D), writes=[Rconst])
        S.op(POOL, lambda e: e.memset(ident[:], 1.0), writes=[Rconst])
        S.op(POOL, lambda e: e.affine_select(out=ident[:], in_=ident[:], pattern=[[-1, 128]],
                                             compare_op=ALU.is_equal, fill=0.0, base=0,
                                             channel_multiplier=1),
             reads=[Rconst], writes=[Rconst])
        S.op(POOL, lambda e: e.memset(onesr0[:], 0.0), writes=[Rconst])
        S.op(POOL, lambda e: e.memset(onesr0[0:1, :], 1.0), writes=[Rconst])
        S.op(POOL, lambda e: e.memset(priv[:], 0.0), writes=priv_all + [Rwmt, Rb0, Rw00, Rdg, Rq, Rqs] + Rstats + Rmv)
        S.op(POOL, lambda e: e.memset(scr[:, NVS - 1, :], 0.0), writes=[Rscr[NVS - 1]])

        early = {"done": False}

        def norm_sq(n, t):
            c0, c1 = TILES[t]
            N = c1 - c0
            for k in range(KC):
                dst = scr[:, k // 4, (k % 4) * 512:(k % 4) * 512 + N]
                src = x[:, k, c0:c1]
                who = ("APDADAPDADADADAD" if (n == 0 and t == 0) else "APAAPAAPAAPAAPAA")[k]
                if who == "P":
                    S.op(POOL, lambda e, dst=dst, src=src: e.tensor_tensor(out=dst, in0=src, in1=src, op=ALU.mult),
                         reads=[Rx[t]], writes=[Rscr[k // 4]])
                elif who == "D":
                    S.op(DVE, lambda e, dst=dst, src=src: e.tensor_tensor(out=dst, in0=src, in1=src, op=ALU.mult),
                         reads=[Rx[t]], writes=[Rscr[k // 4]])
                else:
                    S.op(ACT, lambda e, dst=dst, src=src: e.activation(out=dst, in_=src, func=AF.Square),
                         reads=[Rx[t]], writes=[Rscr[k // 4]])

        def early_sq(n):
            norm_sq(n, 0)
            early["done"] = True

        def norm_tile(n, t, final, skip_sq=False):
            c0, c1 = TILES[t]
            N = c1 - c0
            if not skip_sq:
                norm_sq(n, t)
            b = up_bank()
            fns = []
            for k in range(KC):
                rhs = scr[:, k // 4, (k % 4) * 512:(k % 4) * 512 + N]
                fns.append(lambda e, b=b, rhs=rhs, k=k, N=N: e.matmul(ps[:, b, 0:N], lhsT=onesm[:], rhs=rhs,
                                                                 start=(k == 0), stop=(k == KC - 1)))
            S.group(PE, fns, reads=Rscr[0:4] + [Rconst], writes=[Rps[b]])
            r = tmp_slot()
            S.op(ACT, lambda e, b=b, r=r, N=N: e.activation(out=tmpf[:, r, 0:N], in_=ps[:, b, 0:N], func=AF.Sqrt,
                                                       bias=RMS_EPS),
                 reads=[Rps[b]], writes=[Rtmpf[r]])
            S.op(DVE, lambda e, r=r, N=N: e.reciprocal(out=tmpf[:, r, 0:N], in_=tmpf[:, r, 0:N]),
                 reads=[Rtmpf[r]], writes=[Rtmpf[r]])
            fns = []
            for k in range(KC):
                dst = x[:, k, c0:c1] if final else h[:, k, c0:c1]
                fns.append(lambda e, dst=dst, k=k, r=r, N=N, c0=c0, c1=c1: e.scalar_tensor_tensor(
                    out=dst, in0=x[:, k, c0:c1], scalar=gains[:, n, k:k + 1], in1=tmpf[:, r, 0:N],
                    op0=ALU.mult, op1=ALU.mult))
            if final:
                S.group(DVE, fns, reads=[Rtmpf[r], Rgains, Rx[t]], writes=[Rx[t]])
                S.dma(SP, lambda e, c0=c0, c1=c1: e.dma_start(out=y_o[:, :, c0:c1], in_=x[:, :, c0:c1]),
                      reads=[Rx[t]], owner=Rx[t])
            else:
                S.group(DVE, fns, reads=[Rtmpf[r], Rgains, Rx[t]], writes=[Rh[t]])

        def rms_norm(n, final=False):
            S.flush()
            skip = early["done"]
            early["done"] = False
            norm_tile(n, 0, final, skip_sq=skip)
            if n == 0:
                ws_pump()
            for t in (1, 2):
                S.pending.append(lambda t=t: norm_tile(n, t, final))
            if final:
                S.flush()

        def tiles_of(c0, c1):
            return [t for t, (a, b) in enumerate(TILES) if a < c1 and c0 < b]

        def down_unit(t, par, nk, slot_of, off_of, cols=None):
            c0, c1 = TILES[t] if cols is None else cols
            rxs = [Rx[q] for q in tiles_of(c0, c1)]
            N = c1 - c0
            for fo in range(KC):
                b = dn_bank()
                fns = []
                rslots = set()
                for kf in range(nk):
                    s = slot_of(kf)
                    rslots.add(s)
                    o = off_of(kf) + fo * 128
                    fns.append(lambda e, b=b, s=s, o=o, kf=kf, N=N: e.matmul(
                        ps[:, b, 0:N], lhsT=ring[:, s, o:o + 128], rhs=actb[:, par, kf, 0:N],
                        start=(kf == 0), stop=(kf == nk - 1)))
                S.group(PE, fns, reads=[Ract[par]] + [Rslot[s] for s in sorted(rslots)], writes=[Rps[b]])
                S.op(DVE, lambda e, b=b, fo=fo, N=N, c0=c0, c1=c1: e.tensor_tensor(
                    out=x[:, fo, c0:c1], in0=ps[:, b, 0:N], in1=x[:, fo, c0:c1], op=ALU.add),
                     reads=[Rps[b]], writes=rxs)

        def run_units(units, up_fn, down_fn, hook_after=None, hook=None):
            def dn(i):
                down_fn(units[i], i % 2)
                if hook is not None and units[i] == hook_after:
                    hook()
            for i, u in enumerate(units):
                up_fn(u, i % 2)
                if i > 0:
                    dn(i - 1)
            dn(len(units) - 1)

        def ffn(i):
            blk = {}

            def up_fn(u, par):
                b, t = u
                if t == 0:
                    blk[b] = [ws_take(("up", i, 2 * b)), ws_take(("up", i, 2 * b + 1)), None, None]
                c0, c1 = TILES[t]
                N = c1 - c0
                for f in range(4):
                    s = blk[b][f // 2] % NSLOT
                    bk = up_bank()
                    fns = []
                    for k in range(KC):
                        o = k * 256 + (f % 2) * 128
                        fns.append(lambda e, bk=bk, s=s, o=o, k=k, N=N, c0=c0, c1=c1: e.matmul(
                            ps[:, bk, 0:N], lhsT=ring[:, s, o:o + 128], rhs=h[:, k, c0:c1],
                            start=(k == 0), stop=(k == KC - 1)))
                    S.group(PE, fns, reads=[Rh[t], Rslot[s]], writes=[Rps[bk]])
                    S.op(ACT, lambda e, bk=bk, N=N: e.activation(out=ps[:, bk, 0:N], in_=ps[:, bk, 0:N], func=AF.Relu),
                         reads=[Rps[bk]], writes=[Rps[bk]])
                    S.op(ACT, lambda e, bk=bk, f=f, N=N, par=par: e.activation(out=actb[:, par, f, 0:N], in_=ps[:, bk, 0:N],
                                                                          func=AF.Square),
                         reads=[Rps[bk]], writes=[Ract[par]])
                if t == len(TILES) - 1:
                    ws_release(blk[b][0])
                    ws_release(blk[b][1])

            def down_fn(u, par):
                b, t = u
                if t == 0:
                    blk[b][2] = ws_take(("down", i, 2 * b))
                    blk[b][3] = ws_take(("down", i, 2 * b + 1))
                down_unit(t, par, 4, lambda kf: blk[b][2 + kf // 2] % NSLOT, lambda kf: (kf % 2) * 2048)
                if t == len(TILES) - 1:
                    ws_release(blk[b][2])
                    ws_release(blk[b][3])

            units = [(b, t) for b in range(16) for t in range(len(TILES))]
            run_units(units, up_fn, down_fn, hook_after=(15, 0), hook=lambda: early_sq(2 * i + 2))

        def a_setup(j):
            priv_fence()
            for q in range(2):
                S.dma(SP, lambda e, q=q: e.dma_start(
                    out=scr_f[:, q, :].rearrange("p (g s) -> p g s", g=8),
                    in_=aws_d[j, 8 * q:8 * q + 8].rearrange("g t s -> t g s")),
                      writes=[Rscr[q]])
            for g in range(16):
                q = g // 8
                sl = scr_f[:, q, (g % 8) * 128:(g % 8) * 128 + 128]
                S.op(POOL, lambda e, sl=sl: e.affine_select(out=sl, in_=sl, pattern=[[-1, 128]],
                                                            compare_op=ALU.is_ge, fill=0.0, base=0,
                                                            channel_multiplier=1),
                     reads=[Rscr[q]], writes=[Rscr[q]])
            for qq in range(4):
                bk = up_bank()
                fns = []
                for gl in range(4):
                    g = 4 * qq + gl
                    sl = scr_f[:, g // 8, (g % 8) * 128:(g % 8) * 128 + 128]
                    fns.append(lambda e, bk=bk, gl=gl, sl=sl: e.transpose(ps[:, bk, gl * 128:gl * 128 + 128], sl, ident[:]))
                S.group(PE, fns, reads=[Rscr[0], Rscr[1], Rconst], writes=[Rps[bk]])
                S.op(ACT, lambda e, bk=bk, qq=qq: e.activation(
                    out=wmt[:, 4 * qq:4 * qq + 4, :].rearrange("p g s -> p (g s)"), in_=ps[:, bk, :], func=AF.Copy),
                     reads=[Rps[bk]], writes=[Rwmt])
            S.dma(SP, lambda e: e.dma_start(out=w00[:, :], in_=w00_d[j].partition_broadcast(128)), writes=[Rw00])
            S.dma(SP, lambda e: e.dma_start(out=b0b[:, :], in_=b0_d[j].partition_broadcast(128)), writes=[Rb0])
            for g in range(16):
                S.op(DVE, lambda e, g=g: e.tensor_scalar(out=dg[:, g, :], in0=ident[:, 0:NSMP],
                                                        scalar1=w00[:, g:g + 1], scalar2=None, op0=ALU.mult),
                     reads=[Rw00, Rconst], writes=[Rdg])
            S.op(DVE, lambda e: e.tensor_tensor(out=qs[:, :], in0=lnfm[:, j, 1, :], in1=w00[:, :], op=ALU.mult),
                 reads=[Rlnfm, Rw00], writes=[Rqs])
            S.op(DVE, lambda e: e.tensor_tensor(out=qs[:, :], in0=qs[:, :], in1=b0b[:, :], op=ALU.add),
                 reads=[Rqs, Rb0], writes=[Rqs])
            S.op(POOL, lambda e: e.memset(scr[:, 3, :], 0.0), writes=[Rscr[3]])
            S.dma(POOL, lambda e: e.dma_start(out=scr[0:1, 3, :], in_=abs_d[j:j + 1, :]), writes=[Rscr[3]])
            S.dma(POOL, lambda e: e.dma_start(out=scr[:, 4, :], in_=lnb_d[j].partition_broadcast(128)), writes=[Rscr[4]])
            for qq in range(4):
                bk = up_bank()
                fns = []
                for gl in range(4):
                    g = 4 * qq + gl
                    fns.append(lambda e, bk=bk, gl=gl, g=g: e.matmul(
                        ps[:, bk, gl * 128:(gl + 1) * 128], lhsT=scr[:, 4, g * 128:(g + 1) * 128], rhs=wmt[:, g, :],
                        start=(gl == 0), stop=False, skip_group_check=True))
                    fns.append(lambda e, bk=bk, gl=gl, g=g: e.matmul(
                        ps[:, bk, gl * 128:(gl + 1) * 128], lhsT=onesr0[:], rhs=scr[:, 3, g * 128:(g + 1) * 128],
                        start=False, stop=(gl == 3), skip_group_check=True))
                S.group(PE, fns, reads=[Rscr[3], Rscr[4], Rwmt, Rconst], writes=[Rps[bk]])
                S.op(ACT, lambda e, bk=bk, qq=qq: e.activation(
                    out=qbuf[:, 4 * qq:4 * qq + 4, :].rearrange("p g s -> p (g s)"), in_=ps[:, bk, :], func=AF.Copy),
                     reads=[Rps[bk]], writes=[Rq])

        def a_mixer(i, j):
            a_setup(j)
            nbrow = [0]
            lnrot = [0]

            def a_pass(ptiles):
                pc0, pc1 = ptiles[0][0], ptiles[-1][1]
                chunks = []
                cc = pc0
                while cc < min(pc1, TP):
                    chunks.append((len(chunks), cc, 128))
                    cc += 128
                if pc1 > TP:
                    chunks.append((len(chunks), TP, NSMP))
                assert len(chunks) <= NVS
                rh_of = lambda cc, M: [Rh[q] for q in tiles_of(cc, cc + M)]
                for n in range(8):
                    li = ws_take(("a_in", j, 8 + n))
                    s = li % NSLOT
                    for (vs, cc, M) in chunks:
                        bk = up_bank()
                        fns = []
                        for k in range(KC):
                            fns.append(lambda e, bk=bk, s=s, k=k, cc=cc, M=M: e.matmul(
                                ps[0:M, bk, 0:256], lhsT=h[:, k, cc:cc + M], rhs=ring[:, s, k * 256:(k + 1) * 256],
                                start=(k == 0), stop=(k == KC - 1)))
                        S.group(PE, fns, reads=rh_of(cc, M) + [Rslot[s]], writes=[Rps[bk]])
                        S.op(ACT, lambda e, bk=bk, vs=vs, M=M, n=n: e.activation(
                            out=scr[0:M, vs, n * 256:(n + 1) * 256], in_=ps[0:M, bk, 0:256], func=AF.Gelu),
                             reads=[Rps[bk]], writes=[Rscr[vs]])
                    ws_release(li)
                for (vs, cc, M) in chunks:
                    fns = []
                    for q in range(4):
                        fns.append(lambda e, vs=vs, M=M, q=q: e.bn_stats(out=stats[0:M, vs, q, :],
                                                                       in_=scr[0:M, vs, q * 512:(q + 1) * 512]))
                    S.group(DVE, fns, reads=[Rscr[vs]], writes=[Rstats[vs]])
                    S.op(DVE, lambda e, M=M, vs=vs: e.bn_aggr(out=mv[0:M, vs, 0:2],
                                                           in_=stats[0:M, vs, :, :].rearrange("p a b -> p (a b)")),
                         reads=[Rstats[vs]], writes=[Rmv[vs]])
                for (vs, cc, M) in chunks:
                    S.op(ACT, lambda e, M=M, vs=vs: e.activation(out=mv[0:M, vs, 2:3], in_=mv[0:M, vs, 1:2], func=AF.Sqrt,
                                                              bias=LN_EPS),
                         reads=[Rmv[vs]], writes=[Rmv[vs]])
                for (vs, cc, M) in chunks:
                    S.op(DVE, lambda e, M=M, vs=vs: e.reciprocal(out=mv[0:M, vs, 3:4], in_=mv[0:M, vs, 2:3]),
                         reads=[Rmv[vs]], writes=[Rmv[vs]])
                    S.op(DVE, lambda e, M=M, vs=vs: e.tensor_scalar(out=mv[0:M, vs, 4:5], in0=mv[0:M, vs, 0:1], scalar1=-1.0,
                                                                 scalar2=mv[0:M, vs, 3:4], op0=ALU.mult, op1=ALU.mult),
                         reads=[Rmv[vs]], writes=[Rmv[vs]])
                for (vs, cc, M) in chunks:
                    S.op(ACT, lambda e, vs=vs, M=M: e.activation(
                        out=scr[0:M, vs, :], in_=scr[0:M, vs, :], func=AF.Identity,
                        scale=mv[0:M, vs, 3:4], bias=mv[0:M, vs, 4:5]),
                         reads=[Rscr[vs], Rmv[vs]], writes=[Rscr[vs]])
                tails = [c for c in chunks if (c[2] == NSMP) or (c[1] == TP - 128)]
                tpiece = [0]

                def ln_piece_load(q):
                    lb = q % 2
                    S.dma(SP, lambda e, q=q, lb=lb: e.dma_start(out=lngb[:, lb], in_=lngb_d[j, q].partition_broadcast(128)),
                          writes=[Rlngb[lb]])

                def tail_piece():
                    q = tpiece[0]
                    if not tails or q >= 8:
                        return
                    tpiece[0] += 1
                    lb = q % 2
                    if q + 1 < 8:
                        ln_piece_load(q + 1)
                    cs_ = slice(q * 256, (q + 1) * 256)
                    for (vs, cc, M) in tails:
                        tf = tmp_slot()
                        S.op(DVE, lambda e, vs=vs, M=M, tf=tf, lb=lb, cs_=cs_: e.tensor_tensor(
                            out=tmpf[0:M, tf, 0:256], in0=scr[0:M, vs, cs_], in1=lngb[0:M, lb, 0, :], op=ALU.mult),
                             reads=[Rscr[vs], Rlngb[lb]], writes=[Rtmpf[tf]])
                        S.op(DVE, lambda e, M=M, tf=tf, lb=lb: e.tensor_tensor(
                            out=tmpf[0:M, tf, 0:256], in0=tmpf[0:M, tf, 0:256], in1=lngb[0:M, lb, 1, :], op=ALU.add),
                             reads=[Rtmpf[tf], Rlngb[lb]], writes=[Rtmpf[tf]])
                        dst = vs_o if M == NSMP else v_o
                        S.dma(SP, lambda e, dst=dst, M=M, cs_=cs_, tf=tf: e.dma_start(out=dst[j, :, cs_], in_=tmpf[0:M, tf, 0:256]),
                              reads=[Rtmpf[tf]], owner=Rtmpf[tf])

                if tails:
                    ln_piece_load(0)
                blk = {}
                nt = len(ptiles)

                def up_fn(u, par):
                    gb, ti = u
                    c0, c1 = ptiles[ti]
                    N = c1 - c0
                    if ti == 0:
                        blk[gb] = [ws_take(("a_in", j, 2 * gb)), ws_take(("a_in", j, 2 * gb + 1)), None, None]
                    tch = [c for c in chunks if c0 <= c[1] < c1]
                    npc = len([c for c in tch if c[2] == 128])
                    npr = npc * 128
                    rhs_h = [Rh[q] for q in tiles_of(c0, c1)]
                    for gl in range(4):
                        g = gb * 4 + gl
                        s = blk[gb][gl // 2] % NSLOT
                        bu = up_bank()
                        fns = []
                        for k in range(KC):
                            o = k * 256 + (gl % 2) * 128
                            fns.append(lambda e, bu=bu, s=s, o=o, k=k: e.matmul(
                                ps[:, bu, 0:N], lhsT=ring[:, s, o:o + 128], rhs=h[:, k, c0:c1],
                                start=(k == 0), stop=(k == KC - 1)))
                        S.group(PE, fns, reads=rhs_h + [Rslot[s]], writes=[Rps[bu]])
                        bf = up_bank()
                        fns = []
                        for ci_, (vs, cc, M) in enumerate(tch):
                            po = cc - c0
                            first = (ci_ == 0)
                            last = (ci_ == len(tch) - 1)
                            if M == 128:
                                fns.append(lambda e, bf=bf, vs=vs, g=g, po=po, first=first, last=last: e.matmul(
                                    ps[:, bf, po:po + 128], lhsT=scr[:, vs, g * 128:(g + 1) * 128], rhs=wmt[:, g, :],
                                    start=first, stop=last, skip_group_check=True))
                            else:
                                fns.append(lambda e, bf=bf, vs=vs, g=g, po=po, first=first, last=last: e.matmul(
                                    ps[:, bf, po:po + NSMP], lhsT=scr[:, vs, g * 128:(g + 1) * 128], rhs=dg[:, g, :],
                                    start=first, stop=last, skip_group_check=True))
                        S.group(PE, fns, reads=[Rscr[c[0]] for c in tch] + [Rwmt, Rdg], writes=[Rps[bf]])
                        tf = tmp_slot()
                        S.op(ACT, lambda e, bu=bu, tf=tf: e.activation(out=tmpf[:, tf, 0:N], in_=ps[:, bu, 0:N], func=AF.Gelu),
                             reads=[Rps[bu]], writes=[Rtmpf[tf]])
                        if npc > 0:
                            S.op(DVE, lambda e, bf=bf, g=g, npc=npc, npr=npr: e.scalar_tensor_tensor(
                                out=ps[:, bf, 0:npr].rearrange("p (n t) -> p n t", n=npc),
                                in0=ps[:, bf, 0:npr].rearrange("p (n t) -> p n t", n=npc),
                                scalar=lnfm[:, j, 0, g:g + 1],
                                in1=qbuf[:, g, :].unsqueeze(1).broadcast_to([128, npc, 128]),
                                op0=ALU.mult, op1=ALU.add),
                                 reads=[Rps[bf], Rq, Rlnfm], writes=[Rps[bf]])
                        if N > npr:
                            S.op(DVE, lambda e, bf=bf, g=g, npr=npr: e.tensor_scalar(
                                out=ps[:, bf, npr:N], in0=ps[:, bf, npr:N], scalar1=lnfm[:, j, 0, g:g + 1],
                                scalar2=qs[:, g:g + 1], op0=ALU.mult, op1=ALU.add),
                                 reads=[Rps[bf], Rqs, Rlnfm], writes=[Rps[bf]])
                        S.op(DVE, lambda e, bf=bf, tf=tf, gl=gl, par=par: e.tensor_tensor(
                            out=actb[:, par, gl, 0:N], in0=ps[:, bf, 0:N], in1=tmpf[:, tf, 0:N], op=ALU.mult),
                             reads=[Rps[bf], Rtmpf[tf]], writes=[Ract[par]])
                    if ti == nt - 1:
                        ws_release(blk[gb][0])
                        ws_release(blk[gb][1])
                    tail_piece()

                def down_fn(u, par):
                    gb, ti = u
                    if ti == 0:
                        blk[gb][2] = ws_take(("a_out", j, 2 * gb))
                        blk[gb][3] = ws_take(("a_out", j, 2 * gb + 1))
                    down_unit(None, par, 4, lambda kf: blk[gb][2 + kf // 2] % NSLOT, lambda kf: (kf % 2) * 2048,
                              cols=ptiles[ti])
                    if ti == nt - 1:
                        ws_release(blk[gb][2])
                        ws_release(blk[gb][3])

                units = [(gb, ti) for gb in range(4) for ti in range(nt)]
                run_units(units, up_fn, down_fn)
                while tails and tpiece[0] < 8:
                    tail_piece()

            for ptiles in A_PASSES:
                a_pass(ptiles)

        def b_mixer(i, j):
            priv_fence()
            zflat = lambda fl: scr_f[:, 2 * fl:2 * fl + 2, :].rearrange("p a b -> p (a b)")
            S.dma(SP, lambda e: e.dma_start(out=state, in_=state_d[:, j]), writes=[Rstate])
            for fl in range(2):
                S.op(POOL, lambda e, fl=fl: e.memset(zflat(fl)[:, 0:2], 0.0), writes=[Rscr[2 * fl], Rscr[2 * fl + 1]])
            blk = {}

            def up_fn(u, par):
                bb, t = u
                if t == 0:
                    blk[bb] = [ws_take(("b_in", j, bb)), ws_take(("b_in", j, 8 + bb)), ws_take(("b_in", j, 16 + bb)), None]
                c0, c1 = TILES[t]
                N = c1 - c0
                npr = min(c1, TP) - c0
                for fl in range(2):
                    fc = 2 * bb + fl
                    banks = [None, None, None]
                    for part in (1, 2, 0):
                        s = blk[bb][part] % NSLOT
                        bk = up_bank()
                        banks[part] = bk
                        fns = []
                        for k in range(KC):
                            o = k * 256 + fl * 128
                            fns.append(lambda e, bk=bk, s=s, o=o, k=k: e.matmul(
                                ps[:, bk, 0:N], lhsT=ring[:, s, o:o + 128], rhs=h[:, k, c0:c1],
                                start=(k == 0), stop=(k == KC - 1)))
                        S.group(PE, fns, reads=[Rh[t], Rslot[s]], writes=[Rps[bk]])
                    pB, pC, pH = banks
                    tf = tmp_slot()
                    S.op(ACT, lambda e, pC=pC, tf=tf: e.activation(out=tmpf[:, tf, 0:N], in_=ps[:, pC, 0:N], func=AF.Copy),
                         reads=[Rps[pC]], writes=[Rtmpf[tf]])
                    zf = zflat(fl)
                    Rz = [Rscr[2 * fl], Rscr[2 * fl + 1]]
                    w = lambda kk, fc=fc: convw[:, j, kk, fc:fc + 1]
                    S.op(DVE, lambda e, pH=pH, tf=tf, zf=zf: e.tensor_tensor(
                        out=zf[:, 2 + c0:2 + c0 + npr], in0=ps[:, pH, 0:npr], in1=tmpf[:, tf, 0:npr], op=ALU.mult),
                         reads=[Rps[pH], Rtmpf[tf]], writes=Rz)
                    if N > npr:
                        S.op(DVE, lambda e, pH=pH, tf=tf, fl=fl: e.tensor_tensor(
                            out=zs[:, fl, :], in0=ps[:, pH, npr:N], in1=tmpf[:, tf, npr:N], op=ALU.mult),
                             reads=[Rps[pH], Rtmpf[tf]], writes=[Rzs])
                    S.op(DVE, lambda e, zf=zf, tf=tf, w=w: e.tensor_scalar(
                        out=tmpf[:, tf, 0:npr], in0=zf[:, 2 + c0:2 + c0 + npr], scalar1=w(2), scalar2=None, op0=ALU.mult),
                         reads=Rz + [Rconvw], writes=[Rtmpf[tf]])
                    S.op(DVE, lambda e, zf=zf, tf=tf, w=w: e.scalar_tensor_tensor(
                        out=tmpf[:, tf, 0:npr], in0=zf[:, 1 + c0:1 + c0 + npr], scalar=w(1), in1=tmpf[:, tf, 0:npr],
                        op0=ALU.mult, op1=ALU.add),
                         reads=Rz + [Rconvw, Rtmpf[tf]], writes=[Rtmpf[tf]])
                    S.op(DVE, lambda e, zf=zf, tf=tf, w=w: e.scalar_tensor_tensor(
                        out=tmpf[:, tf, 0:npr], in0=zf[:, c0:c0 + npr], scalar=w(0), in1=tmpf[:, tf, 0:npr],
                        op0=ALU.mult, op1=ALU.add),
                         reads=Rz + [Rconvw, Rtmpf[tf]], writes=[Rtmpf[tf]])
                    if N > npr:
                        S.op(DVE, lambda e, fl=fl, tf=tf, w=w: e.tensor_scalar(
                            out=tmpf[:, tf, npr:N], in0=zs[:, fl, :], scalar1=w(2), scalar2=None, op0=ALU.mult),
                             reads=[Rzs, Rconvw], writes=[Rtmpf[tf]])
                        S.op(DVE, lambda e, tf=tf, fc=fc, w=w: e.scalar_tensor_tensor(
                            out=tmpf[:, tf, npr:N], in0=state[:, fc, 1, :], scalar=w(1), in1=tmpf[:, tf, npr:N],
                            op0=ALU.mult, op1=ALU.add),
                             reads=[Rstate, Rconvw, Rtmpf[tf]], writes=[Rtmpf[tf]])
                        S.op(DVE, lambda e, tf=tf, fc=fc, w=w: e.scalar_tensor_tensor(
                            out=tmpf[:, tf, npr:N], in0=state[:, fc, 0, :], scalar=w(0), in1=tmpf[:, tf, npr:N],
                            op0=ALU.mult, op1=ALU.add),
                             reads=[Rstate, Rconvw, Rtmpf[tf]], writes=[Rtmpf[tf]])
                        S.op(POOL, lambda e, fl=fl, fc=fc: e.tensor_copy(out=csz[:, fc, :], in_=zs[:, fl, :]),
                             reads=[Rzs], writes=[Rcsz])
                        S.op(POOL, lambda e, zf=zf, fc=fc: e.tensor_copy(out=cpst[:, fc, :], in_=zf[:, TP:TP + 2]),
                             reads=Rz, writes=[Rcp])
                    S.op(DVE, lambda e, pB=pB, tf=tf, fl=fl, par=par: e.tensor_tensor(
                        out=actb[:, par, fl, 0:N], in0=ps[:, pB, 0:N], in1=tmpf[:, tf, 0:N], op=ALU.mult),
                         reads=[Rps[pB], Rtmpf[tf]], writes=[Ract[par]])
                if t == len(TILES) - 1:
                    for q in range(3):
                        ws_release(blk[bb][q])

            def down_fn(u, par):
                bb, t = u
                if t == 0:
                    blk[bb][3] = ws_take(("b_out", j, bb))
                down_unit(t, par, 2, lambda kf: blk[bb][3] % NSLOT, lambda kf: kf * 2048)
                if t == len(TILES) - 1:
                    ws_release(blk[bb][3])

            units = [(bb, t) for bb in range(8) for t in range(len(TILES))]
            run_units(units, up_fn, down_fn)
            S.dma(SP, lambda e: e.dma_start(out=cp_o[:, j], in_=cpst), reads=[Rcp], owner=Rcp)
            S.dma(SP, lambda e: e.dma_start(out=cs_o[:, j, :, 1, :], in_=csz), reads=[Rcsz], owner=Rcsz)
            S.dma(SP, lambda e: e.dma_start(out=cs_o[:, j, :, 0, :], in_=state[:, :, 1, :]), reads=[Rstate], owner=Rstate)

        for i in range(n_layers):
            j = i // 2
            rms_norm(2 * i)
            if debug and i == 0:
                for t_ in range(3):
                    c0_, c1_ = TILES[t_]
                    S.dma(POOL, lambda e, c0_=c0_, c1_=c1_: e.dma_start(out=dbg_h[:, :, c0_:c1_], in_=h[:, :, c0_:c1_]),
                          reads=[Rh[t_]], owner=Rh[t_])
            if i % 2 == 0:
                a_mixer(i, j)
            else:
                b_mixer(i, j)
            if debug and i == 0:
                for t_ in range(3):
                    c0_, c1_ = TILES[t_]
                    S.dma(SP, lambda e, c0_=c0_, c1_=c1_: e.dma_start(out=dbg_x[:, :, c0_:c1_], in_=x[:, :, c0_:c1_]),
                          reads=[Rx[t_]], owner=Rx[t_])
            rms_norm(2 * i + 1)
            ffn(i)
        rms_norm(8, final=True)
        S.flush()
        for R in S.all_dma_res:
            SP.ops.append(lambda e, R=R: e.wait_ge(R.dsem, R.dcnt))

        stat = {k: len(v.ops) for k, v in (("pe", PE), ("act", ACT), ("dve", DVE), ("pool", POOL), ("sp", SP))}
        print("[kernel] ops per engine:", stat, "loads", ws["next"], "/", len(loads), flush=True)
        with nc.Block() as block:
            @block.tensor
            def _(e):
                for f in PE.ops:
                    f(e)

            @block.scalar
            def _(e):
                for f in ACT.ops:
                    f(e)

            @block.vector
            def _(e):
                for f in DVE.ops:
                    f(e)

            @block.gpsimd
            def _(e):
                for f in POOL.ops:
                    f(e)

            @block.sync
            def _(e):
                for f in SP.ops:
                    f(e)
    return nc


def _cblocks(W):
    K, N = W.shape
    assert K == D
    return W.reshape(KC, 128, N // 256, 256).transpose(2, 1, 0, 3).reshape(N // 256, 128, 4096)


def _rblocks(W):
    K, N = W.shape
    assert N == D
    return W.reshape(K // 256, 2, 128, D).transpose(0, 2, 1, 3).reshape(K // 256, 128, 4096)


def _build_wall(a_w_in, a_w_out, b_w_in, b_w_out, ffn_w_up, ffn_w_down):
    wall = np.empty((NWALL, 128, 4096), dtype=np.float32)
    for j in range(2):
        s = WALL_IDX[("a_in", j, 0)]; wall[s:s + 16] = _cblocks(a_w_in[j])
        s = WALL_IDX[("a_out", j, 0)]; wall[s:s + 8] = _rblocks(a_w_out[j])
        s = WALL_IDX[("b_in", j, 0)]; wall[s:s + 24] = _cblocks(b_w_in[j])
        s = WALL_IDX[("b_out", j, 0)]; wall[s:s + 8] = _rblocks(b_w_out[j])
    for i in range(DEPTH):
        s = WALL_IDX[("up", i, 0)]; wall[s:s + 32] = _cblocks(ffn_w_up[i])
        s = WALL_IDX[("down", i, 0)]; wall[s:s + 32] = _rblocks(ffn_w_down[i])
    return wall


_NC_CACHE = {}


def _prepare(x_prompt, x_sample, state_conv, norm_mix_g, norm_ffn_g, a_w_in, a_ln_g, a_ln_b, a_w_s,
             a_b_s, a_w_out, b_w_in, b_conv_w, b_w_out, ffn_w_up, ffn_w_down, final_norm_g, cores=range(8)):
    f = lambda a: np.ascontiguousarray(np.asarray(a, dtype=np.float32))
    x_prompt, x_sample, state_conv = f(x_prompt), f(x_sample), f(state_conv)
    norm_mix_g, norm_ffn_g, final_norm_g = f(norm_mix_g), f(norm_ffn_g), f(final_norm_g)
    a_w_in, a_ln_g, a_ln_b, a_w_s, a_b_s, a_w_out = f(a_w_in), f(a_ln_g), f(a_ln_b), f(a_w_s), f(a_b_s), f(a_w_out)
    b_w_in, b_conv_w, b_w_out, ffn_w_up, ffn_w_down = f(b_w_in), f(b_conv_w), f(b_w_out), f(ffn_w_up), f(ffn_w_down)

    wall = _build_wall(a_w_in, a_w_out, b_w_in, b_w_out, ffn_w_up, ffn_w_down)
    gl = []
    for i in range(DEPTH):
        gl.append(norm_mix_g[i]); gl.append(norm_ffn_g[i])
    gl.append(final_norm_g)
    gains = np.ascontiguousarray(np.stack(gl, 0).reshape(9, KC, 128).transpose(2, 0, 1))
    convw = np.ascontiguousarray(b_conv_w.reshape(2, 3, KC, 128).transpose(3, 0, 1, 2))
    lngb = np.ascontiguousarray(np.stack([a_ln_g.reshape(2, 8, 256), a_ln_b.reshape(2, 8, 256)], axis=2))
    abs_ = np.ascontiguousarray(a_b_s.reshape(2, D))
    w00 = np.ascontiguousarray(a_w_s[:, :, 0, 0])
    b0 = np.ascontiguousarray(a_b_s[:, :, 0])
    lnfm = np.ascontiguousarray(np.stack([a_ln_g.reshape(2, KC, 128), a_ln_b.reshape(2, KC, 128)], axis=1).transpose(3, 0, 1, 2))

    in_maps = []
    for c in cores:
        b, half = c // 2, c % 2
        t0 = 0 if half == 0 else 2048 - TP
        xt = np.concatenate([x_prompt[b, t0:t0 + TP, :], x_sample[NSMP * c:NSMP * (c + 1), 0, :]], axis=0)
        xin = np.ascontiguousarray(xt.reshape(T, KC, 128).transpose(2, 1, 0))
        st = state_conv[:, NSMP * c:NSMP * (c + 1)]
        st = np.ascontiguousarray(st.reshape(2, NSMP, 2, KC, 128).transpose(4, 0, 3, 2, 1))
        in_maps.append({"xin": xin, "wall": wall, "gains": gains, "convw": convw, "state": st,
                        "lngb": lngb, "aws": a_w_s, "abs": abs_, "w00": w00, "b0": b0, "lnfm": lnfm, "lnb": a_ln_b})
    return in_maps


def _assemble(R):
    y_prompt = np.empty((4, 2048, D), np.float32)
    y_sample = np.empty((128, 1, D), np.float32)
    v_prompt = np.empty((2, 4, 128, D), np.float32)
    v_sample = np.empty((2, 128, 1, D), np.float32)
    conv_prompt = np.empty((2, 4, 2, D), np.float32)
    conv_sample = np.empty((2, 128, 2, D), np.float32)
    for c in range(8):
        b, half = c // 2, c % 2
        r = R[c]
        yt = np.asarray(r["y_fm"]).transpose(2, 1, 0).reshape(T, D)
        if half == 0:
            y_prompt[b, 0:TP] = yt[0:TP]
        else:
            y_prompt[b, TP:2048] = yt[2 * TP - 2048:TP]
        y_sample[NSMP * c:NSMP * (c + 1), 0] = yt[TP:T]
        v_sample[:, NSMP * c:NSMP * (c + 1), 0] = np.asarray(r["vs_out"])
        cs = np.asarray(r["convs"])
        conv_sample[:, NSMP * c:NSMP * (c + 1)] = cs.transpose(1, 4, 3, 2, 0).reshape(2, NSMP, 2, D)
        if half == 1:
            v_prompt[:, b] = np.asarray(r["v_out"])
            cp = np.asarray(r["convp"])
            conv_prompt[:, b] = cp.transpose(1, 3, 2, 0).reshape(2, 2, D)
    return (y_prompt, y_sample, v_prompt, v_sample, conv_prompt, conv_sample)


def kernel(**inputs):
    in_maps = _prepare(**inputs)
    if "nc" not in _NC_CACHE:
        _NC_CACHE["nc"] = build_program()
    nc = _NC_CACHE["nc"]
    res = run_bass_kernel_spmd(nc, in_maps, core_ids=list(range(8)))
    return _assemble(res.results)
```

```python
import numpy as np
from contextlib import ExitStack
import concourse.bass as bass
import concourse.mybir as mybir
from concourse.bass_utils import run_bass_kernel_spmd

F32 = mybir.dt.float32
BF16 = mybir.dt.bfloat16
ALU = mybir.AluOpType
AF = mybir.ActivationFunctionType

D = 2048
KC = 16
NCH = 9
TP = NCH * 128
NSMP = 16
T = TP + NSMP
TILES = [(0, 512), (512, 1024), (1024, T)]
A_PASSES = [[(0, 384), (384, 640)], [(640, 1024), (1024, T)]]
NVS = 5
DEPTH = 4
DFF = 8192
NSLOT = 6
RMS_EPS = 1e-6
LN_EPS = 1e-5


def _wall_index():
    idx = {}
    n = 0
    for j in range(2):
        for b in range(16):
            idx[("a_in", j, b)] = n; n += 1
        for r in range(8):
            idx[("a_out", j, r)] = n; n += 1
        for b in range(24):
            idx[("b_in", j, b)] = n; n += 1
        for r in range(8):
            idx[("b_out", j, r)] = n; n += 1
    for i in range(DEPTH):
        for b in range(32):
            idx[("up", i, b)] = n; n += 1
        for r in range(32):
            idx[("down", i, r)] = n; n += 1
    return idx, n


WALL_IDX, NWALL = _wall_index()


def _load_sequence():
    seq = []
    for i in range(DEPTH):
        j = i // 2
        if i % 2 == 0:
            for _t in range(len(A_PASSES)):
                for n in range(8):
                    seq.append(("a_in", j, 8 + n))
                for gb in range(4):
                    seq.append(("a_in", j, 2 * gb))
                    seq.append(("a_in", j, 2 * gb + 1))
                    seq.append(("a_out", j, 2 * gb))
                    seq.append(("a_out", j, 2 * gb + 1))
        else:
            for bb in range(8):
                seq.append(("b_in", j, bb))
                seq.append(("b_in", j, 8 + bb))
                seq.append(("b_in", j, 16 + bb))
                seq.append(("b_out", j, bb))
        for b in range(16):
            seq.append(("up", i, 2 * b))
            seq.append(("up", i, 2 * b + 1))
            seq.append(("down", i, 2 * b))
            seq.append(("down", i, 2 * b + 1))
    return seq


class Res:
    __slots__ = ("name", "w", "r", "dsem", "dcnt")

    def __init__(self, name):
        self.name = name
        self.w = None
        self.r = {}
        self.dsem = None
        self.dcnt = 0


class Eng:
    def __init__(self, name, sem, is_pe=False):
        self.name = name
        self.sem = sem
        self.cnt = 0
        self.ops = []
        self.known = {}
        self.is_pe = is_pe


class Sched:
    def __init__(self, nc, stack):
        self.nc = nc
        self.stack = stack
        mk = lambda n: stack.enter_context(nc.semaphore(n))
        self.pe = Eng("pe", mk("s_pe"), True)
        self.act = Eng("act", mk("s_act"))
        self.dve = Eng("dve", mk("s_dve"))
        self.pool = Eng("pool", mk("s_pool"))
        self.sp = Eng("sp", mk("s_sp"))
        self.nsem = 0
        self.all_dma_res = []
        self.pending = []
        self.in_pending = False
        self.guarded = set()

    def _waits(self, E, reads, writes):
        need = {}

        def add(ev, same_ok):
            if ev is None:
                return
            sem, val, owner = ev
            if owner is E and same_ok:
                return
            k = id(sem)
            if k not in need or need[k][1] < val:
                need[k] = (sem, val)

        strict = (E.name == "pool")
        for R in reads:
            add(R.w, E.is_pe)
        for R in writes:
            add(R.w, not strict)
            for ev in R.r.values():
                add(ev, not strict)
        for k, (sem, val) in need.items():
            if E.known.get(k, 0) >= val:
                continue
            E.known[k] = val
            E.ops.append(lambda e, sem=sem, val=val: e.wait_ge(sem, val))

    def _guard(self, reads, writes):
        if self.pending and not self.in_pending:
            for R in list(reads) + list(writes):
                if R in self.guarded:
                    self.flush()
                    return

    def flush(self):
        while self.pending:
            self._run_pending()

    def _run_pending(self):
        f = self.pending.pop(0)
        self.in_pending = True
        try:
            f()
        finally:
            self.in_pending = False

    def op(self, E, fn, reads=(), writes=()):
        self._guard(reads, writes)
        self._waits(E, reads, writes)
        E.cnt += 1
        sem = E.sem
        E.ops.append(lambda e, fn=fn, sem=sem: fn(e).then_inc(sem, 1))
        ev = (sem, E.cnt, E)
        for R in reads:
            R.r[E.name] = ev
        for R in writes:
            R.w = ev
            R.r = {}

    def group(self, E, fns, reads=(), writes=()):
        self._guard(reads, writes)
        self._waits(E, reads, writes)
        E.cnt += 1
        sem = E.sem
        for fn in fns[:-1]:
            E.ops.append(lambda e, fn=fn: fn(e))
        last = fns[-1]
        E.ops.append(lambda e, fn=last, sem=sem: fn(e).then_inc(sem, 1))
        ev = (sem, E.cnt, E)
        for R in reads:
            R.r[E.name] = ev
        for R in writes:
            R.w = ev
            R.r = {}
        if E.is_pe and self.pending and not self.in_pending:
            self._run_pending()

    def dma(self, Q, fn, reads=(), writes=(), owner=None):
        if owner is None:
            owner = writes[0] if writes else reads[0]
        self._guard(reads, writes)
        if owner.dsem is None:
            owner.dsem = self.stack.enter_context(self.nc.semaphore("d%d" % self.nsem))
            self.nsem += 1
            self.all_dma_res.append(owner)
        self._waits(Q, reads, writes)
        owner.dcnt += 16
        sem = owner.dsem
        Q.ops.append(lambda e, fn=fn, sem=sem: fn(e).then_inc(sem, 16))
        ev = (sem, owner.dcnt, None)
        for R in reads:
            R.r["dma%d" % id(sem)] = ev
        for R in writes:
            R.w = ev
            R.r = {}


def build_program(n_layers=DEPTH, debug=False):
    nc = bass.Bass("TRN2", target_bir_lowering=False)
    dram_in = lambda n, s: nc.dram_tensor(n, s, F32, kind="ExternalInput").ap()
    dram_out = lambda n, s: nc.dram_tensor(n, s, F32, kind="ExternalOutput").ap()

    xin = dram_in("xin", [128, KC, T])
    wall = dram_in("wall", [NWALL, 128, 4096])
    gains_d = dram_in("gains", [128, 9, KC])
    convw_d = dram_in("convw", [128, 2, 3, KC])
    state_d = dram_in("state", [128, 2, KC, 2, NSMP])
    lngb_d = dram_in("lngb", [2, 8, 2, 256])
    aws_d = dram_in("aws", [2, 16, 128, 128])
    abs_d = dram_in("abs", [2, D])
    w00_d = dram_in("w00", [2, 16])
    b0_d = dram_in("b0", [2, 16])
    lnfm_d = dram_in("lnfm", [128, 2, 2, KC])
    lnb_d = dram_in("lnb", [2, D])

    y_o = dram_out("y_fm", [128, KC, T])
    v_o = dram_out("v_out", [2, 128, D])
    vs_o = dram_out("vs_out", [2, NSMP, D])
    cp_o = dram_out("convp", [128, 2, KC, 2])
    cs_o = dram_out("convs", [128, 2, KC, 2, NSMP])
    if debug:
        dbg_h = dram_out("dbg_h", [128, KC, T])
        dbg_v = dram_out("dbg_v", [128, 4, 2048])
        dbg_x = dram_out("dbg_x", [128, KC, T])

    with ExitStack() as stack:
        sb = lambda n, s, d: stack.enter_context(nc.sbuf_tensor(n, s, d))
        S = Sched(nc, stack)
        PE, ACT, DVE, POOL, SP = S.pe, S.act, S.dve, S.pool, S.sp

        x = sb("x", [128, KC, T], F32)
        h = sb("h", [128, KC, T], BF16)
        ring = sb("ring", [128, NSLOT, 4096], BF16)
        scr = sb("scr", [128, NVS, 2048], BF16)
        scr_f = scr.bitcast(F32)
        actb = sb("actb", [128, 2, 4, 512], BF16)
        actb_f = actb.bitcast(F32).rearrange("p a b c -> p (a b c)")
        tmpf = sb("tmpf", [128, 3, 512], F32)
        priv = sb("priv", [128, 3408], F32)
        privb = priv.bitcast(BF16)
        lngb = priv[:, 0:1024].rearrange("p (u a b) -> p u a b", u=2, a=2)
        wmt = privb[:, 2048:4096].rearrange("p (g s) -> p g s", g=16)
        qbuf = privb[:, 4096:6144].rearrange("p (g s) -> p g s", g=16)
        dg = privb[:, 6144:6400].rearrange("p (g s) -> p g s", g=16)
        w00 = priv[:, 3200:3216]
        b0b = priv[:, 3216:3232]
        qs = priv[:, 3232:3248]
        stats = priv[:, 3248:3368].rearrange("p (c a b) -> p c a b", c=5, a=4)
        mv = priv[:, 3368:3408].rearrange("p (a b) -> p a b", a=5)
        state = priv[:, 0:512].rearrange("p (a b c) -> p a b c", a=KC, b=2)
        csz = priv[:, 512:768].rearrange("p (a b) -> p a b", a=KC)
        zs = priv[:, 768:800].rearrange("p (a b) -> p a b", a=2)
        cpst = priv[:, 800:832].rearrange("p (a b) -> p a b", a=KC)
        gains = sb("gains_s", [128, 9, KC], F32)
        lnfm = sb("lnfm_s", [128, 2, 2, KC], F32)
        convw = sb("convw_s", [128, 2, 3, KC], F32)
        onesm = sb("onesm", [128, 128], BF16)
        onesr0 = sb("onesr0", [128, 128], BF16)
        ident = sb("ident", [128, 128], F32)
        dummy = sb("dummyt", [128, 8], F32)
        ps = stack.enter_context(nc.psum_tensor("ps", [128, 8, 512], F32))

        Rx = [Res("x%d" % t) for t in range(3)]
        Rh = [Res("h%d" % t) for t in range(3)]
        Rslot = [Res("slot%d" % s) for s in range(NSLOT)]
        Rscr = [Res("scr%d" % s) for s in range(NVS)]
        S.guarded = {Rx[1], Rx[2], Rh[1], Rh[2]} | set(Rscr)
        Ract = [Res("act0"), Res("act1")]
        Rtmpf = [Res("tmpf%d" % i) for i in range(3)]
        Rlngb = [Res("lngb0"), Res("lngb1")]
        Rwmt = Res("wmt")
        Rconst = Res("const")
        Rgains = Res("gains")
        Rconvw = Res("convw")
        Rstate = Res("state")
        Rcp = Res("cpst")
        Rcsz = Res("csz")
        Rzs = Res("zs")
        Rq = Res("qbuf")
        Rqs = Res("qs")
        Rb0 = Res("b0")
        Rlnfm = Res("lnfm")
        Rw00 = Res("w00")
        Rdg = Res("dg")
        Rstats = [Res("stats%d" % c) for c in range(5)]
        Rmv = [Res("mv%d" % c) for c in range(5)]
        Rdummy = Res("dummy")
        Rps = [Res("ps%d" % b) for b in range(8)]
        priv_all = Rlngb + [Rstate, Rcp, Rcsz, Rzs]

        rot = {"up": 0, "dn": 0, "tmpf": 0}

        def nxt(kind, n, base=0):
            v = rot[kind]
            rot[kind] = (v + 1) % n
            return base + v

        up_bank = lambda: nxt("up", 4, 0)
        dn_bank = lambda: nxt("dn", 4, 4)
        tmp_slot = lambda: nxt("tmpf", 3)

        def priv_fence():
            S.op(POOL, lambda e: e.memset(dummy[:], 0.0), writes=priv_all + [Rdummy])

        loads = _load_sequence()
        ws = {"next": 0, "released": [False] * len(loads), "cursor": 0}

        def ws_pump():
            while ws["next"] < len(loads):
                i = ws["next"]
                if i >= NSLOT and not ws["released"][i - NSLOT]:
                    break
                s = i % NSLOT
                widx = WALL_IDX[loads[i]]
                S.dma(POOL, lambda e, s=s, widx=widx: e.dma_start(out=ring[:, s, :], in_=wall[widx]),
                      writes=[Rslot[s]], owner=Rslot[s])
                ws["next"] += 1

        def ws_take(key):
            i = ws["cursor"]
            assert loads[i] == key, (loads[i], key)
            assert i < ws["next"], "weight load %d not issued yet" % i
            ws["cursor"] += 1
            return i

        def ws_release(i):
            ws["released"][i] = True
            ws_pump()

        S.dma(SP, lambda e: e.dma_start(out=gains[:], in_=gains_d), writes=[Rgains])
        for t, (c0, c1) in enumerate(TILES):
            S.dma(SP, lambda e, c0=c0, c1=c1: e.dma_start(out=x[:, :, c0:c1], in_=xin[:, :, c0:c1]),
                  writes=[Rx[t]])
        S.dma(SP, lambda e: e.dma_start(out=convw[:], in_=convw_d), writes=[Rconvw])
        S.dma(SP, lambda e: e.dma_start(out=lnfm[:], in_=lnfm_d), writes=[Rlnfm])
        S.op(POOL, lambda e: e.memset(onesm[:], 1.0 / D), writes=[Rconst])
        S.op(POOL, lambda e: e.memset(ident[:], 1.0), writes=[Rconst])
        S.op(POOL, lambda e: e.affine_select(out=ident[:], in_=ident[:], pattern=[[-1, 128]],
                                             compare_op=ALU.is_equal, fill=0.0, base=0,
                                             channel_multiplier=1),
             reads=[Rconst], writes=[Rconst])
        S.op(POOL, lambda e: e.memset(onesr0[:], 0.0), writes=[Rconst])
        S.op(POOL, lambda e: e.memset(onesr0[0:1, :], 1.0), writes=[Rconst])
        S.op(POOL, lambda e: e.memset(priv[:], 0.0), writes=priv_all + [Rwmt, Rb0, Rw00, Rdg, Rq, Rqs] + Rstats + Rmv)
        S.op(POOL, lambda e: e.memset(scr[:, NVS - 1, :], 0.0), writes=[Rscr[NVS - 1]])

        early = {"done": False}

        def norm_sq(n, t):
            c0, c1 = TILES[t]
            N = c1 - c0
            for k in range(KC):
                dst = scr[:, k // 4, (k % 4) * 512:(k % 4) * 512 + N]
                src = x[:, k, c0:c1]
                who = ("AAAADDDDAAAAPPPP" if (n == 0 and t == 0) else "AAAAAAAAAAAAPPPP")[k]
                if who == "P":
                    S.op(POOL, lambda e, dst=dst, src=src: e.tensor_tensor(out=dst, in0=src, in1=src, op=ALU.mult),
                         reads=[Rx[t]], writes=[Rscr[k // 4]])
                elif who == "D":
                    S.op(DVE, lambda e, dst=dst, src=src: e.tensor_tensor(out=dst, in0=src, in1=src, op=ALU.mult),
                         reads=[Rx[t]], writes=[Rscr[k // 4]])
                else:
                    S.op(ACT, lambda e, dst=dst, src=src: e.activation(out=dst, in_=src, func=AF.Square),
                         reads=[Rx[t]], writes=[Rscr[k // 4]])

        def early_sq(n):
            norm_sq(n, 0)
            early["done"] = True

        def norm_tile(n, t, final, skip_sq=False):
            c0, c1 = TILES[t]
            N = c1 - c0
            if not skip_sq:
                norm_sq(n, t)
            b = up_bank()
            fns = []
            for k in range(KC):
                rhs = scr[:, k // 4, (k % 4) * 512:(k % 4) * 512 + N]
                fns.append(lambda e, b=b, rhs=rhs, k=k, N=N: e.matmul(ps[:, b, 0:N], lhsT=onesm[:], rhs=rhs,
                                                                 start=(k == 0), stop=(k == KC - 1)))
            S.group(PE, fns, reads=Rscr[0:4] + [Rconst], writes=[Rps[b]])
            r = tmp_slot()
            S.op(ACT, lambda e, b=b, r=r, N=N: e.activation(out=tmpf[:, r, 0:N], in_=ps[:, b, 0:N], func=AF.Sqrt,
                                                       bias=RMS_EPS),
                 reads=[Rps[b]], writes=[Rtmpf[r]])
            S.op(DVE, lambda e, r=r, N=N: e.reciprocal(out=tmpf[:, r, 0:N], in_=tmpf[:, r, 0:N]),
                 reads=[Rtmpf[r]], writes=[Rtmpf[r]])
            fns = []
            for k in range(KC):
                dst = x[:, k, c0:c1] if final else h[:, k, c0:c1]
                fns.append(lambda e, dst=dst, k=k, r=r, N=N, c0=c0, c1=c1: e.scalar_tensor_tensor(
                    out=dst, in0=x[:, k, c0:c1], scalar=gains[:, n, k:k + 1], in1=tmpf[:, r, 0:N],
                    op0=ALU.mult, op1=ALU.mult))
            if final:
                S.group(DVE, fns, reads=[Rtmpf[r], Rgains, Rx[t]], writes=[Rx[t]])
                S.dma(SP, lambda e, c0=c0, c1=c1: e.dma_start(out=y_o[:, :, c0:c1], in_=x[:, :, c0:c1]),
                      reads=[Rx[t]], owner=Rx[t])
            else:
                S.group(DVE, fns, reads=[Rtmpf[r], Rgains, Rx[t]], writes=[Rh[t]])

        def rms_norm(n, final=False):
            S.flush()
            skip = early["done"]
            early["done"] = False
            norm_tile(n, 0, final, skip_sq=skip)
            if n == 0:
                ws_pump()
            for t in (1, 2):
                S.pending.append(lambda t=t: norm_tile(n, t, final))
            if final:
                S.flush()

        def tiles_of(c0, c1):
            return [t for t, (a, b) in enumerate(TILES) if a < c1 and c0 < b]

        def down_unit(t, par, nk, slot_of, off_of, cols=None):
            c0, c1 = TILES[t] if cols is None else cols
            rxs = [Rx[q] for q in tiles_of(c0, c1)]
            N = c1 - c0
            for fo in range(KC):
                b = dn_bank()
                fns = []
                rslots = set()
                for kf in range(nk):
                    s = slot_of(kf)
                    rslots.add(s)
                    o = off_of(kf) + fo * 128
                    fns.append(lambda e, b=b, s=s, o=o, kf=kf, N=N: e.matmul(
                        ps[:, b, 0:N], lhsT=ring[:, s, o:o + 128], rhs=actb[:, par, kf, 0:N],
                        start=(kf == 0), stop=(kf == nk - 1)))
                S.group(PE, fns, reads=[Ract[par]] + [Rslot[s] for s in sorted(rslots)], writes=[Rps[b]])
                S.op(DVE, lambda e, b=b, fo=fo, N=N, c0=c0, c1=c1: e.tensor_tensor(
                    out=x[:, fo, c0:c1], in0=ps[:, b, 0:N], in1=x[:, fo, c0:c1], op=ALU.add),
                     reads=[Rps[b]], writes=rxs)

        def run_units(units, up_fn, down_fn, hook_after=None, hook=None):
            def dn(i):
                down_fn(units[i], i % 2)
                if hook is not None and units[i] == hook_after:
                    hook()
            for i, u in enumerate(units):
                up_fn(u, i % 2)
                if i > 0:
                    dn(i - 1)
            dn(len(units) - 1)

        def ffn(i):
            blk = {}

            def up_fn(u, par):
                b, t = u
                if t == 0:
                    blk[b] = [ws_take(("up", i, 2 * b)), ws_take(("up", i, 2 * b + 1)), None, None]
                c0, c1 = TILES[t]
                N = c1 - c0
                for f in range(4):
                    s = blk[b][f // 2] % NSLOT
                    bk = up_bank()
                    fns = []
                    for k in range(KC):
                        o = k * 256 + (f % 2) * 128
                        fns.append(lambda e, bk=bk, s=s, o=o, k=k, N=N, c0=c0, c1=c1: e.matmul(
                            ps[:, bk, 0:N], lhsT=ring[:, s, o:o + 128], rhs=h[:, k, c0:c1],
                            start=(k == 0), stop=(k == KC - 1)))
                    S.group(PE, fns, reads=[Rh[t], Rslot[s]], writes=[Rps[bk]])
                    S.op(ACT, lambda e, bk=bk, N=N: e.activation(out=ps[:, bk, 0:N], in_=ps[:, bk, 0:N], func=AF.Relu),
                         reads=[Rps[bk]], writes=[Rps[bk]])
                    S.op(ACT, lambda e, bk=bk, f=f, N=N, par=par: e.activation(out=actb[:, par, f, 0:N], in_=ps[:, bk, 0:N],
                                                                          func=AF.Square),
                         reads=[Rps[bk]], writes=[Ract[par]])
                if t == len(TILES) - 1:
                    ws_release(blk[b][0])
                    ws_release(blk[b][1])

            def down_fn(u, par):
                b, t = u
                if t == 0:
                    blk[b][2] = ws_take(("down", i, 2 * b))
                    blk[b][3] = ws_take(("down", i, 2 * b + 1))
                down_unit(t, par, 4, lambda kf: blk[b][2 + kf // 2] % NSLOT, lambda kf: (kf % 2) * 2048)
                if t == len(TILES) - 1:
                    ws_release(blk[b][2])
                    ws_release(blk[b][3])

            units = [(b, t) for b in range(16) for t in range(len(TILES))]
            run_units(units, up_fn, down_fn, hook_after=(15, 0), hook=lambda: early_sq(2 * i + 2))

        def a_setup(j):
            priv_fence()
            for q in range(2):
                S.dma(SP, lambda e, q=q: e.dma_start(
                    out=scr_f[:, q, :].rearrange("p (g s) -> p g s", g=8),
                    in_=aws_d[j, 8 * q:8 * q + 8].rearrange("g t s -> t g s")),
                      writes=[Rscr[q]])
            for g in range(16):
                q = g // 8
                sl = scr_f[:, q, (g % 8) * 128:(g % 8) * 128 + 128]
                S.op(POOL, lambda e, sl=sl: e.affine_select(out=sl, in_=sl, pattern=[[-1, 128]],
                                                            compare_op=ALU.is_ge, fill=0.0, base=0,
                                                            channel_multiplier=1),
                     reads=[Rscr[q]], writes=[Rscr[q]])
            for qq in range(4):
                bk = up_bank()
                fns = []
                for gl in range(4):
                    g = 4 * qq + gl
                    sl = scr_f[:, g // 8, (g % 8) * 128:(g % 8) * 128 + 128]
                    fns.append(lambda e, bk=bk, gl=gl, sl=sl: e.transpose(ps[:, bk, gl * 128:gl * 128 + 128], sl, ident[:]))
                S.group(PE, fns, reads=[Rscr[0], Rscr[1], Rconst], writes=[Rps[bk]])
                S.op(ACT, lambda e, bk=bk, qq=qq: e.activation(
                    out=wmt[:, 4 * qq:4 * qq + 4, :].rearrange("p g s -> p (g s)"), in_=ps[:, bk, :], func=AF.Copy),
                     reads=[Rps[bk]], writes=[Rwmt])
            S.dma(SP, lambda e: e.dma_start(out=w00[:, :], in_=w00_d[j].partition_broadcast(128)), writes=[Rw00])
            S.dma(SP, lambda e: e.dma_start(out=b0b[:, :], in_=b0_d[j].partition_broadcast(128)), writes=[Rb0])
            for g in range(16):
                S.op(DVE, lambda e, g=g: e.tensor_scalar(out=dg[:, g, :], in0=ident[:, 0:NSMP],
                                                        scalar1=w00[:, g:g + 1], scalar2=None, op0=ALU.mult),
                     reads=[Rw00, Rconst], writes=[Rdg])
            S.op(DVE, lambda e: e.tensor_tensor(out=qs[:, :], in0=lnfm[:, j, 1, :], in1=w00[:, :], op=ALU.mult),
                 reads=[Rlnfm, Rw00], writes=[Rqs])
            S.op(DVE, lambda e: e.tensor_tensor(out=qs[:, :], in0=qs[:, :], in1=b0b[:, :], op=ALU.add),
                 reads=[Rqs, Rb0], writes=[Rqs])
            S.op(POOL, lambda e: e.memset(scr[:, 3, :], 0.0), writes=[Rscr[3]])
            S.dma(POOL, lambda e: e.dma_start(out=scr[0:1, 3, :], in_=abs_d[j:j + 1, :]), writes=[Rscr[3]])
            S.dma(POOL, lambda e: e.dma_start(out=scr[:, 4, :], in_=lnb_d[j].partition_broadcast(128)), writes=[Rscr[4]])
            for qq in range(4):
                bk = up_bank()
                fns = []
                for gl in range(4):
                    g = 4 * qq + gl
                    fns.append(lambda e, bk=bk, gl=gl, g=g: e.matmul(
                        ps[:, bk, gl * 128:(gl + 1) * 128], lhsT=scr[:, 4, g * 128:(g + 1) * 128], rhs=wmt[:, g, :],
                        start=(gl == 0), stop=False, skip_group_check=True))
                    fns.append(lambda e, bk=bk, gl=gl, g=g: e.matmul(
                        ps[:, bk, gl * 128:(gl + 1) * 128], lhsT=onesr0[:], rhs=scr[:, 3, g * 128:(g + 1) * 128],
                        start=False, stop=(gl == 3), skip_group_check=True))
                S.group(PE, fns, reads=[Rscr[3], Rscr[4], Rwmt, Rconst], writes=[Rps[bk]])
                S.op(ACT, lambda e, bk=bk, qq=qq: e.activation(
                    out=qbuf[:, 4 * qq:4 * qq + 4, :].rearrange("p g s -> p (g s)"), in_=ps[:, bk, :], func=AF.Copy),
                     reads=[Rps[bk]], writes=[Rq])

        def a_mixer(i, j):
            a_setup(j)
            nbrow = [0]
            lnrot = [0]

            def a_pass(ptiles):
                pc0, pc1 = ptiles[0][0], ptiles[-1][1]
                chunks = []
                cc = pc0
                while cc < min(pc1, TP):
                    chunks.append((len(chunks), cc, 128))
                    cc += 128
                if pc1 > TP:
                    chunks.append((len(chunks), TP, NSMP))
                assert len(chunks) <= NVS
                rh_of = lambda cc, M: [Rh[q] for q in tiles_of(cc, cc + M)]
                for n in range(8):
                    li = ws_take(("a_in", j, 8 + n))
                    s = li % NSLOT
                    for (vs, cc, M) in chunks:
                        bk = up_bank()
                        fns = []
                        for k in range(KC):
                            fns.append(lambda e, bk=bk, s=s, k=k, cc=cc, M=M: e.matmul(
                                ps[0:M, bk, 0:256], lhsT=h[:, k, cc:cc + M], rhs=ring[:, s, k * 256:(k + 1) * 256],
                                start=(k == 0), stop=(k == KC - 1)))
                        S.group(PE, fns, reads=rh_of(cc, M) + [Rslot[s]], writes=[Rps[bk]])
                        S.op(ACT, lambda e, bk=bk, vs=vs, M=M, n=n: e.activation(
                            out=scr[0:M, vs, n * 256:(n + 1) * 256], in_=ps[0:M, bk, 0:256], func=AF.Gelu),
                             reads=[Rps[bk]], writes=[Rscr[vs]])
                    ws_release(li)
                for (vs, cc, M) in chunks:
                    fns = []
                    for q in range(4):
                        fns.append(lambda e, vs=vs, M=M, q=q: e.bn_stats(out=stats[0:M, vs, q, :],
                                                                       in_=scr[0:M, vs, q * 512:(q + 1) * 512]))
                    S.group(DVE, fns, reads=[Rscr[vs]], writes=[Rstats[vs]])
                    S.op(DVE, lambda e, M=M, vs=vs: e.bn_aggr(out=mv[0:M, vs, 0:2],
                                                           in_=stats[0:M, vs, :, :].rearrange("p a b -> p (a b)")),
                         reads=[Rstats[vs]], writes=[Rmv[vs]])
                for (vs, cc, M) in chunks:
                    S.op(ACT, lambda e, M=M, vs=vs: e.activation(out=mv[0:M, vs, 2:3], in_=mv[0:M, vs, 1:2], func=AF.Sqrt,
                                                              bias=LN_EPS),
                         reads=[Rmv[vs]], writes=[Rmv[vs]])
                for (vs, cc, M) in chunks:
                    S.op(DVE, lambda e, M=M, vs=vs: e.reciprocal(out=mv[0:M, vs, 3:4], in_=mv[0:M, vs, 2:3]),
                         reads=[Rmv[vs]], writes=[Rmv[vs]])
                    S.op(DVE, lambda e, M=M, vs=vs: e.tensor_scalar(out=mv[0:M, vs, 4:5], in0=mv[0:M, vs, 0:1], scalar1=-1.0,
                                                                 scalar2=mv[0:M, vs, 3:4], op0=ALU.mult, op1=ALU.mult),
                         reads=[Rmv[vs]], writes=[Rmv[vs]])
                for (vs, cc, M) in chunks:
                    S.op(ACT, lambda e, vs=vs, M=M: e.activation(
                        out=scr[0:M, vs, :], in_=scr[0:M, vs, :], func=AF.Identity,
                        scale=mv[0:M, vs, 3:4], bias=mv[0:M, vs, 4:5]),
                         reads=[Rscr[vs], Rmv[vs]], writes=[Rscr[vs]])
                tails = [c for c in chunks if (c[2] == NSMP) or (c[1] == TP - 128)]
                tpiece = [0]

                def ln_piece_load(q):
                    lb = q % 2
                    S.dma(SP, lambda e, q=q, lb=lb: e.dma_start(out=lngb[:, lb], in_=lngb_d[j, q].partition_broadcast(128)),
                          writes=[Rlngb[lb]])

                def tail_piece():
                    q = tpiece[0]
                    if not tails or q >= 8:
                        return
                    tpiece[0] += 1
                    lb = q % 2
                    if q + 1 < 8:
                        ln_piece_load(q + 1)
                    cs_ = slice(q * 256, (q + 1) * 256)
                    for (vs, cc, M) in tails:
                        tf = tmp_slot()
                        S.op(DVE, lambda e, vs=vs, M=M, tf=tf, lb=lb, cs_=cs_: e.tensor_tensor(
                            out=tmpf[0:M, tf, 0:256], in0=scr[0:M, vs, cs_], in1=lngb[0:M, lb, 0, :], op=ALU.mult),
                             reads=[Rscr[vs], Rlngb[lb]], writes=[Rtmpf[tf]])
                        S.op(DVE, lambda e, M=M, tf=tf, lb=lb: e.tensor_tensor(
                            out=tmpf[0:M, tf, 0:256], in0=tmpf[0:M, tf, 0:256], in1=lngb[0:M, lb, 1, :], op=ALU.add),
                             reads=[Rtmpf[tf], Rlngb[lb]], writes=[Rtmpf[tf]])
                        dst = vs_o if M == NSMP else v_o
                        S.dma(SP, lambda e, dst=dst, M=M, cs_=cs_, tf=tf: e.dma_start(out=dst[j, :, cs_], in_=tmpf[0:M, tf, 0:256]),
                              reads=[Rtmpf[tf]], owner=Rtmpf[tf])

                if tails:
                    ln_piece_load(0)
                blk = {}
                nt = len(ptiles)

                def up_fn(u, par):
                    gb, ti = u
                    c0, c1 = ptiles[ti]
                    N = c1 - c0
                    if ti == 0:
                        blk[gb] = [ws_take(("a_in", j, 2 * gb)), ws_take(("a_in", j, 2 * gb + 1)), None, None]
                    tch = [c for c in chunks if c0 <= c[1] < c1]
                    npc = len([c for c in tch if c[2] == 128])
                    npr = npc * 128
                    rhs_h = [Rh[q] for q in tiles_of(c0, c1)]
                    for gl in range(4):
                        g = gb * 4 + gl
                        s = blk[gb][gl // 2] % NSLOT
                        bu = up_bank()
                        fns = []
                        for k in range(KC):
                            o = k * 256 + (gl % 2) * 128
                            fns.append(lambda e, bu=bu, s=s, o=o, k=k: e.matmul(
                                ps[:, bu, 0:N], lhsT=ring[:, s, o:o + 128], rhs=h[:, k, c0:c1],
                                start=(k == 0), stop=(k == KC - 1)))
                        S.group(PE, fns, reads=rhs_h + [Rslot[s]], writes=[Rps[bu]])
                        bf = up_bank()
                        fns = []
                        for ci_, (vs, cc, M) in enumerate(tch):
                            po = cc - c0
                            first = (ci_ == 0)
                            last = (ci_ == len(tch) - 1)
                            if M == 128:
                                fns.append(lambda e, bf=bf, vs=vs, g=g, po=po, first=first, last=last: e.matmul(
                                    ps[:, bf, po:po + 128], lhsT=scr[:, vs, g * 128:(g + 1) * 128], rhs=wmt[:, g, :],
                                    start=first, stop=last, skip_group_check=True))
                            else:
                                fns.append(lambda e, bf=bf, vs=vs, g=g, po=po, first=first, last=last: e.matmul(
                                    ps[:, bf, po:po + NSMP], lhsT=scr[:, vs, g * 128:(g + 1) * 128], rhs=dg[:, g, :],
                                    start=first, stop=last, skip_group_check=True))
                        S.group(PE, fns, reads=[Rscr[c[0]] for c in tch] + [Rwmt, Rdg], writes=[Rps[bf]])
                        tf = tmp_slot()
                        S.op(ACT, lambda e, bu=bu, tf=tf: e.activation(out=tmpf[:, tf, 0:N], in_=ps[:, bu, 0:N], func=AF.Gelu),
                             reads=[Rps[bu]], writes=[Rtmpf[tf]])
                        if npc > 0:
                            S.op(DVE, lambda e, bf=bf, g=g, npc=npc, npr=npr: e.scalar_tensor_tensor(
                                out=ps[:, bf, 0:npr].rearrange("p (n t) -> p n t", n=npc),
                                in0=ps[:, bf, 0:npr].rearrange("p (n t) -> p n t", n=npc),
                                scalar=lnfm[:, j, 0, g:g + 1],
                                in1=qbuf[:, g, :].unsqueeze(1).broadcast_to([128, npc, 128]),
                                op0=ALU.mult, op1=ALU.add),
                                 reads=[Rps[bf], Rq, Rlnfm], writes=[Rps[bf]])
                        if N > npr:
                            S.op(DVE, lambda e, bf=bf, g=g, npr=npr: e.tensor_scalar(
                                out=ps[:, bf, npr:N], in0=ps[:, bf, npr:N], scalar1=lnfm[:, j, 0, g:g + 1],
                                scalar2=qs[:, g:g + 1], op0=ALU.mult, op1=ALU.add),
                                 reads=[Rps[bf], Rqs, Rlnfm], writes=[Rps[bf]])
                        S.op(DVE, lambda e, bf=bf, tf=tf, gl=gl, par=par: e.tensor_tensor(
                            out=actb[:, par, gl, 0:N], in0=ps[:, bf, 0:N], in1=tmpf[:, tf, 0:N], op=ALU.mult),
                             reads=[Rps[bf], Rtmpf[tf]], writes=[Ract[par]])
                    if ti == nt - 1:
                        ws_release(blk[gb][0])
                        ws_release(blk[gb][1])
                    tail_piece()

                def down_fn(u, par):
                    gb, ti = u
                    if ti == 0:
                        blk[gb][2] = ws_take(("a_out", j, 2 * gb))
                        blk[gb][3] = ws_take(("a_out", j, 2 * gb + 1))
                    down_unit(None, par, 4, lambda kf: blk[gb][2 + kf // 2] % NSLOT, lambda kf: (kf % 2) * 2048,
                              cols=ptiles[ti])
                    if ti == nt - 1:
                        ws_release(blk[gb][2])
                        ws_release(blk[gb][3])

                units = [(gb, ti) for gb in range(4) for ti in range(nt)]
                run_units(units, up_fn, down_fn)
                while tails and tpiece[0] < 8:
                    tail_piece()

            for ptiles in A_PASSES:
                a_pass(ptiles)

        def b_mixer(i, j):
            priv_fence()
            zflat = lambda fl: scr_f[:, 2 * fl:2 * fl + 2, :].rearrange("p a b -> p (a b)")
            S.dma(SP, lambda e: e.dma_start(out=state, in_=state_d[:, j]), writes=[Rstate])
            for fl in range(2):
                S.op(POOL, lambda e, fl=fl: e.memset(zflat(fl)[:, 0:2], 0.0), writes=[Rscr[2 * fl], Rscr[2 * fl + 1]])
            blk = {}

            def up_fn(u, par):
                bb, t = u
                if t == 0:
                    blk[bb] = [ws_take(("b_in", j, bb)), ws_take(("b_in", j, 8 + bb)), ws_take(("b_in", j, 16 + bb)), None]
                c0, c1 = TILES[t]
                N = c1 - c0
                npr = min(c1, TP) - c0
                for fl in range(2):
                    fc = 2 * bb + fl
                    banks = [None, None, None]
                    for part in (1, 2, 0):
                        s = blk[bb][part] % NSLOT
                        bk = up_bank()
                        banks[part] = bk
                        fns = []
                        for k in range(KC):
                            o = k * 256 + fl * 128
                            fns.append(lambda e, bk=bk, s=s, o=o, k=k: e.matmul(
                                ps[:, bk, 0:N], lhsT=ring[:, s, o:o + 128], rhs=h[:, k, c0:c1],
                                start=(k == 0), stop=(k == KC - 1)))
                        S.group(PE, fns, reads=[Rh[t], Rslot[s]], writes=[Rps[bk]])
                    pB, pC, pH = banks
                    tf = tmp_slot()
                    S.op(ACT, lambda e, pC=pC, tf=tf: e.activation(out=tmpf[:, tf, 0:N], in_=ps[:, pC, 0:N], func=AF.Copy),
                         reads=[Rps[pC]], writes=[Rtmpf[tf]])
                    zf = zflat(fl)
                    Rz = [Rscr[2 * fl], Rscr[2 * fl + 1]]
                    w = lambda kk, fc=fc: convw[:, j, kk, fc:fc + 1]
                    S.op(DVE, lambda e, pH=pH, tf=tf, zf=zf: e.tensor_tensor(
                        out=zf[:, 2 + c0:2 + c0 + npr], in0=ps[:, pH, 0:npr], in1=tmpf[:, tf, 0:npr], op=ALU.mult),
                         reads=[Rps[pH], Rtmpf[tf]], writes=Rz)
                    if N > npr:
                        S.op(DVE, lambda e, pH=pH, tf=tf, fl=fl: e.tensor_tensor(
                            out=zs[:, fl, :], in0=ps[:, pH, npr:N], in1=tmpf[:, tf, npr:N], op=ALU.mult),
                             reads=[Rps[pH], Rtmpf[tf]], writes=[Rzs])
                    S.op(DVE, lambda e, zf=zf, tf=tf, w=w: e.tensor_scalar(
                        out=tmpf[:, tf, 0:npr], in0=zf[:, 2 + c0:2 + c0 + npr], scalar1=w(2), scalar2=None, op0=ALU.mult),
                         reads=Rz + [Rconvw], writes=[Rtmpf[tf]])
                    S.op(DVE, lambda e, zf=zf, tf=tf, w=w: e.scalar_tensor_tensor(
                        out=tmpf[:, tf, 0:npr], in0=zf[:, 1 + c0:1 + c0 + npr], scalar=w(1), in1=tmpf[:, tf, 0:npr],
                        op0=ALU.mult, op1=ALU.add),
                         reads=Rz + [Rconvw, Rtmpf[tf]], writes=[Rtmpf[tf]])
                    S.op(DVE, lambda e, zf=zf, tf=tf, w=w: e.scalar_tensor_tensor(
                        out=tmpf[:, tf, 0:npr], in0=zf[:, c0:c0 + npr], scalar=w(0), in1=tmpf[:, tf, 0:npr],
                        op0=ALU.mult, op1=ALU.add),
                         reads=Rz + [Rconvw, Rtmpf[tf]], writes=[Rtmpf[tf]])
                    if N > npr:
                        S.op(DVE, lambda e, fl=fl, tf=tf, w=w: e.tensor_scalar(
                            out=tmpf[:, tf, npr:N], in0=zs[:, fl, :], scalar1=w(2), scalar2=None, op0=ALU.mult),
                             reads=[Rzs, Rconvw], writes=[Rtmpf[tf]])
                        S.op(DVE, lambda e, tf=tf, fc=fc, w=w: e.scalar_tensor_tensor(
                            out=tmpf[:, tf, npr:N], in0=state[:, fc, 1, :], scalar=w(1), in1=tmpf[:, tf, npr:N],
                            op0=ALU.mult, op1=ALU.add),
                             reads=[Rstate, Rconvw, Rtmpf[tf]], writes=[Rtmpf[tf]])
                        S.op(DVE, lambda e, tf=tf, fc=fc, w=w: e.scalar_tensor_tensor(
                            out=tmpf[:, tf, npr:N], in0=state[:, fc, 0, :], scalar=w(0), in1=tmpf[:, tf, npr:N],
                            op0=ALU.mult, op1=ALU.add),
                             reads=[Rstate, Rconvw, Rtmpf[tf]], writes=[Rtmpf[tf]])
                        S.op(POOL, lambda e, fl=fl, fc=fc: e.tensor_copy(out=csz[:, fc, :], in_=zs[:, fl, :]),
                             reads=[Rzs], writes=[Rcsz])
                        S.op(POOL, lambda e, zf=zf, fc=fc: e.tensor_copy(out=cpst[:, fc, :], in_=zf[:, TP:TP + 2]),
                             reads=Rz, writes=[Rcp])
                    S.op(DVE, lambda e, pB=pB, tf=tf, fl=fl, par=par: e.tensor_tensor(
                        out=actb[:, par, fl, 0:N], in0=ps[:, pB, 0:N], in1=tmpf[:, tf, 0:N], op=ALU.mult),
                         reads=[Rps[pB], Rtmpf[tf]], writes=[Ract[par]])
                if t == len(TILES) - 1:
                    for q in range(3):
                        ws_release(blk[bb][q])

            def down_fn(u, par):
                bb, t = u
                if t == 0:
                    blk[bb][3] = ws_take(("b_out", j, bb))
                down_unit(t, par, 2, lambda kf: blk[bb][3] % NSLOT, lambda kf: kf * 2048)
                if t == len(TILES) - 1:
                    ws_release(blk[bb][3])

            units = [(bb, t) for bb in range(8) for t in range(len(TILES))]
            run_units(units, up_fn, down_fn)
            S.dma(SP, lambda e: e.dma_start(out=cp_o[:, j], in_=cpst), reads=[Rcp], owner=Rcp)
            S.dma(SP, lambda e: e.dma_start(out=cs_o[:, j, :, 1, :], in_=csz), reads=[Rcsz], owner=Rcsz)
            S.dma(SP, lambda e: e.dma_start(out=cs_o[:, j, :, 0, :], in_=state[:, :, 1, :]), reads=[Rstate], owner=Rstate)

        for i in range(n_layers):
            j = i // 2
            rms_norm(2 * i)
            if debug and i == 0:
                for t_ in range(3):
                    c0_, c1_ = TILES[t_]
                    S.dma(POOL, lambda e, c0_=c0_, c1_=c1_: e.dma_start(out=dbg_h[:, :, c0_:c1_], in_=h[:, :, c0_:c1_]),
                          reads=[Rh[t_]], owner=Rh[t_])
            if i % 2 == 0:
                a_mixer(i, j)
            else:
                b_mixer(i, j)
            if debug and i == 0:
                for t_ in range(3):
                    c0_, c1_ = TILES[t_]
                    S.dma(SP, lambda e, c0_=c0_, c1_=c1_: e.dma_start(out=dbg_x[:, :, c0_:c1_], in_=x[:, :, c0_:c1_]),
                          reads=[Rx[t_]], owner=Rx[t_])
            rms_norm(2 * i + 1)
            ffn(i)
        rms_norm(8, final=True)
        S.flush()
        for R in S.all_dma_res:
            SP.ops.append(lambda e, R=R: e.wait_ge(R.dsem, R.dcnt))

        stat = {k: len(v.ops) for k, v in (("pe", PE), ("act", ACT), ("dve", DVE), ("pool", POOL), ("sp", SP))}
        print("[kernel] ops per engine:", stat, "loads", ws["next"], "/", len(loads), flush=True)
        with nc.Block() as block:
            @block.tensor
            def _(e):
                for f in PE.ops:
                    f(e)

            @block.scalar
            def _(e):
                for f in ACT.ops:
                    f(e)

            @block.vector
            def _(e):
                for f in DVE.ops:
                    f(e)

            @block.gpsimd
            def _(e):
                for f in POOL.ops:
                    f(e)

            @block.sync
            def _(e):
                for f in SP.ops:
                    f(e)
    return nc


def _cblocks(W):
    K, N = W.shape
    assert K == D
    return W.reshape(KC, 128, N // 256, 256).transpose(2, 1, 0, 3).reshape(N // 256, 128, 4096)


def _rblocks(W):
    K, N = W.shape
    assert N == D
    return W.reshape(K // 256, 2, 128, D).transpose(0, 2, 1, 3).reshape(K // 256, 128, 4096)


def _build_wall(a_w_in, a_w_out, b_w_in, b_w_out, ffn_w_up, ffn_w_down):
    wall = np.empty((NWALL, 128, 4096), dtype=np.float32)
    for j in range(2):
        s = WALL_IDX[("a_in", j, 0)]; wall[s:s + 16] = _cblocks(a_w_in[j])
        s = WALL_IDX[("a_out", j, 0)]; wall[s:s + 8] = _rblocks(a_w_out[j])
        s = WALL_IDX[("b_in", j, 0)]; wall[s:s + 24] = _cblocks(b_w_in[j])
        s = WALL_IDX[("b_out", j, 0)]; wall[s:s + 8] = _rblocks(b_w_out[j])
    for i in range(DEPTH):
        s = WALL_IDX[("up", i, 0)]; wall[s:s + 32] = _cblocks(ffn_w_up[i])
        s = WALL_IDX[("down", i, 0)]; wall[s:s + 32] = _rblocks(ffn_w_down[i])
    return wall


_NC_CACHE = {}


def _prepare(x_prompt, x_sample, state_conv, norm_mix_g, norm_ffn_g, a_w_in, a_ln_g, a_ln_b, a_w_s,
             a_b_s, a_w_out, b_w_in, b_conv_w, b_w_out, ffn_w_up, ffn_w_down, final_norm_g, cores=range(8)):
    f = lambda a: np.ascontiguousarray(np.asarray(a, dtype=np.float32))
    x_prompt, x_sample, state_conv = f(x_prompt), f(x_sample), f(state_conv)
    norm_mix_g, norm_ffn_g, final_norm_g = f(norm_mix_g), f(norm_ffn_g), f(final_norm_g)
    a_w_in, a_ln_g, a_ln_b, a_w_s, a_b_s, a_w_out = f(a_w_in), f(a_ln_g), f(a_ln_b), f(a_w_s), f(a_b_s), f(a_w_out)
    b_w_in, b_conv_w, b_w_out, ffn_w_up, ffn_w_down = f(b_w_in), f(b_conv_w), f(b_w_out), f(ffn_w_up), f(ffn_w_down)

    wall = _build_wall(a_w_in, a_w_out, b_w_in, b_w_out, ffn_w_up, ffn_w_down)
    gl = []
    for i in range(DEPTH):
        gl.append(norm_mix_g[i]); gl.append(norm_ffn_g[i])
    gl.append(final_norm_g)
    gains = np.ascontiguousarray(np.stack(gl, 0).reshape(9, KC, 128).transpose(2, 0, 1))
    convw = np.ascontiguousarray(b_conv_w.reshape(2, 3, KC, 128).transpose(3, 0, 1, 2))
    lngb = np.ascontiguousarray(np.stack([a_ln_g.reshape(2, 8, 256), a_ln_b.reshape(2, 8, 256)], axis=2))
    abs_ = np.ascontiguousarray(a_b_s.reshape(2, D))
    w00 = np.ascontiguousarray(a_w_s[:, :, 0, 0])
    b0 = np.ascontiguousarray(a_b_s[:, :, 0])
    lnfm = np.ascontiguousarray(np.stack([a_ln_g.reshape(2, KC, 128), a_ln_b.reshape(2, KC, 128)], axis=1).transpose(3, 0, 1, 2))

    in_maps = []
    for c in cores:
        b, half = c // 2, c % 2
        t0 = 0 if half == 0 else 2048 - TP
        xt = np.concatenate([x_prompt[b, t0:t0 + TP, :], x_sample[NSMP * c:NSMP * (c + 1), 0, :]], axis=0)
        xin = np.ascontiguousarray(xt.reshape(T, KC, 128).transpose(2, 1, 0))
        st = state_conv[:, NSMP * c:NSMP * (c + 1)]
        st = np.ascontiguousarray(st.reshape(2, NSMP, 2, KC, 128).transpose(4, 0, 3, 2, 1))
        in_maps.append({"xin": xin, "wall": wall, "gains": gains, "convw": convw, "state": st,
                        "lngb": lngb, "aws": a_w_s, "abs": abs_, "w00": w00, "b0": b0, "lnfm": lnfm, "lnb": a_ln_b})
    return in_maps


def _assemble(R):
    y_prompt = np.empty((4, 2048, D), np.float32)
    y_sample = np.empty((128, 1, D), np.float32)
    v_prompt = np.empty((2, 4, 128, D), np.float32)
    v_sample = np.empty((2, 128, 1, D), np.float32)
    conv_prompt = np.empty((2, 4, 2, D), np.float32)
    conv_sample = np.empty((2, 128, 2, D), np.float32)
    for c in range(8):
        b, half = c // 2, c % 2
        r = R[c]
        yt = np.asarray(r["y_fm"]).transpose(2, 1, 0).reshape(T, D)
        if half == 0:
            y_prompt[b, 0:TP] = yt[0:TP]
        else:
            y_prompt[b, TP:2048] = yt[2 * TP - 2048:TP]
        y_sample[NSMP * c:NSMP * (c + 1), 0] = yt[TP:T]
        v_sample[:, NSMP * c:NSMP * (c + 1), 0] = np.asarray(r["vs_out"])
        cs = np.asarray(r["convs"])
        conv_sample[:, NSMP * c:NSMP * (c + 1)] = cs.transpose(1, 4, 3, 2, 0).reshape(2, NSMP, 2, D)
        if half == 1:
            v_prompt[:, b] = np.asarray(r["v_out"])
            cp = np.asarray(r["convp"])
            conv_prompt[:, b] = cp.transpose(1, 3, 2, 0).reshape(2, 2, D)
    return (y_prompt, y_sample, v_prompt, v_sample, conv_prompt, conv_sample)


def kernel(**inputs):
    in_maps = _prepare(**inputs)
    if "nc" not in _NC_CACHE:
        _NC_CACHE["nc"] = build_program()
    nc = _NC_CACHE["nc"]
    res = run_bass_kernel_spmd(nc, in_maps, core_ids=list(range(8)))
    return _assemble(res.results)
```

```python
import numpy as np
from contextlib import ExitStack
import concourse.bass as bass
import concourse.mybir as mybir
from concourse.bass_utils import run_bass_kernel_spmd

F32 = mybir.dt.float32
BF16 = mybir.dt.bfloat16
ALU = mybir.AluOpType
AF = mybir.ActivationFunctionType

D = 2048
KC = 16
NCH = 9
TP = NCH * 128
NSMP = 16
T = TP + NSMP
TILES = [(0, 512), (512, 1024), (1024, T)]
A_PASSES = [[(0, 384), (384, 640)], [(640, 1024), (1024, T)]]
NVS = 5
DEPTH = 4
DFF = 8192
NSLOT = 6
RMS_EPS = 1e-6
LN_EPS = 1e-5


def _wall_index():
    idx = {}
    n = 0
    for j in range(2):
        for b in range(16):
            idx[("a_in", j, b)] = n; n += 1
        for r in range(8):
            idx[("a_out", j, r)] = n; n += 1
        for b in range(24):
            idx[("b_in", j, b)] = n; n += 1
        for r in range(8):
            idx[("b_out", j, r)] = n; n += 1
    for i in range(DEPTH):
        for b in range(32):
            idx[("up", i, b)] = n; n += 1
        for r in range(32):
            idx[("down", i, r)] = n; n += 1
    return idx, n


WALL_IDX, NWALL = _wall_index()


def _load_sequence():
    seq = []
    for i in range(DEPTH):
        j = i // 2
        if i % 2 == 0:
            for _t in range(len(A_PASSES)):
                for n in range(8):
                    seq.append(("a_in", j, 8 + n))
                for gb in range(4):
                    seq.append(("a_in", j, 2 * gb))
                    seq.append(("a_in", j, 2 * gb + 1))
                    seq.append(("a_out", j, 2 * gb))
                    seq.append(("a_out", j, 2 * gb + 1))
        else:
            for bb in range(8):
                seq.append(("b_in", j, bb))
                seq.append(("b_in", j, 8 + bb))
                seq.append(("b_in", j, 16 + bb))
                seq.append(("b_out", j, bb))
        for b in range(16):
            seq.append(("up", i, 2 * b))
            seq.append(("up", i, 2 * b + 1))
            seq.append(("down", i, 2 * b))
            seq.append(("down", i, 2 * b + 1))
    return seq


class Res:
    __slots__ = ("name", "w", "r", "dsem", "dcnt")

    def __init__(self, name):
        self.name = name
        self.w = None
        self.r = {}
        self.dsem = None
        self.dcnt = 0


class Eng:
    def __init__(self, name, sem, is_pe=False):
        self.name = name
        self.sem = sem
        self.cnt = 0
        self.ops = []
        self.known = {}
        self.is_pe = is_pe


class Sched:
    def __init__(self, nc, stack):
        self.nc = nc
        self.stack = stack
        mk = lambda n: stack.enter_context(nc.semaphore(n))
        self.pe = Eng("pe", mk("s_pe"), True)
        self.act = Eng("act", mk("s_act"))
        self.dve = Eng("dve", mk("s_dve"))
        self.pool = Eng("pool", mk("s_pool"))
        self.sp = Eng("sp", mk("s_sp"))
        self.nsem = 0
        self.all_dma_res = []
        self.pending = []
        self.in_pending = False
        self.guarded = set()

    def _waits(self, E, reads, writes):
        need = {}

        def add(ev, same_ok):
            if ev is None:
                return
            sem, val, owner = ev
            if owner is E and same_ok:
                return
            k = id(sem)
            if k not in need or need[k][1] < val:
                need[k] = (sem, val)

        strict = (E.name == "pool")
        for R in reads:
            add(R.w, E.is_pe)
        for R in writes:
            add(R.w, not strict)
            for ev in R.r.values():
                add(ev, not strict)
        for k, (sem, val) in need.items():
            if E.known.get(k, 0) >= val:
                continue
            E.known[k] = val
            E.ops.append(lambda e, sem=sem, val=val: e.wait_ge(sem, val))

    def _guard(self, reads, writes):
        if self.pending and not self.in_pending:
            for R in list(reads) + list(writes):
                if R in self.guarded:
                    self.flush()
                    return

    def flush(self):
        while self.pending:
            self._run_pending()

    def _run_pending(self):
        f = self.pending.pop(0)
        self.in_pending = True
        try:
            f()
        finally:
            self.in_pending = False

    def op(self, E, fn, reads=(), writes=()):
        self._guard(reads, writes)
        self._waits(E, reads, writes)
        E.cnt += 1
        sem = E.sem
        E.ops.append(lambda e, fn=fn, sem=sem: fn(e).then_inc(sem, 1))
        ev = (sem, E.cnt, E)
        for R in reads:
            R.r[E.name] = ev
        for R in writes:
            R.w = ev
            R.r = {}

    def group(self, E, fns, reads=(), writes=()):
        self._guard(reads, writes)
        self._waits(E, reads, writes)
        E.cnt += 1
        sem = E.sem
        for fn in fns[:-1]:
            E.ops.append(lambda e, fn=fn: fn(e))
        last = fns[-1]
        E.ops.append(lambda e, fn=last, sem=sem: fn(e).then_inc(sem, 1))
        ev = (sem, E.cnt, E)
        for R in reads:
            R.r[E.name] = ev
        for R in writes:
            R.w = ev
            R.r = {}
        if E.is_pe and self.pending and not self.in_pending:
            self._run_pending()

    def dma(self, Q, fn, reads=(), writes=(), owner=None):
        if owner is None:
            owner = writes[0] if writes else reads[0]
        self._guard(reads, writes)
        if owner.dsem is None:
            owner.dsem = self.stack.enter_context(self.nc.semaphore("d%d" % self.nsem))
            self.nsem += 1
            self.all_dma_res.append(owner)
        self._waits(Q, reads, writes)
        owner.dcnt += 16
        sem = owner.dsem
        Q.ops.append(lambda e, fn=fn, sem=sem: fn(e).then_inc(sem, 16))
        ev = (sem, owner.dcnt, None)
        for R in reads:
            R.r["dma%d" % id(sem)] = ev
        for R in writes:
            R.w = ev
            R.r = {}


def build_program(n_layers=DEPTH, debug=False):
    nc = bass.Bass("TRN2", target_bir_lowering=False)
    dram_in = lambda n, s: nc.dram_tensor(n, s, F32, kind="ExternalInput").ap()
    dram_out = lambda n, s: nc.dram_tensor(n, s, F32, kind="ExternalOutput").ap()

    xin = dram_in("xin", [128, KC, T])
    wall = dram_in("wall", [NWALL, 128, 4096])
    gains_d = dram_in("gains", [128, 9, KC])
    convw_d = dram_in("convw", [128, 2, 3, KC])
    state_d = dram_in("state", [128, 2, KC, 2, NSMP])
    lngb_d = dram_in("lngb", [2, 8, 2, 256])
    aws_d = dram_in("aws", [2, 16, 128, 128])
    abs_d = dram_in("abs", [2, D])
    w00_d = dram_in("w00", [2, 16])
    b0_d = dram_in("b0", [2, 16])
    lnfm_d = dram_in("lnfm", [128, 2, 2, KC])
    lnb_d = dram_in("lnb", [2, D])

    y_o = dram_out("y_fm", [128, KC, T])
    v_o = dram_out("v_out", [2, 128, D])
    vs_o = dram_out("vs_out", [2, NSMP, D])
    cp_o = dram_out("convp", [128, 2, KC, 2])
    cs_o = dram_out("convs", [128, 2, KC, 2, NSMP])
    if debug:
        dbg_h = dram_out("dbg_h", [128, KC, T])
        dbg_v = dram_out("dbg_v", [128, 4, 2048])
        dbg_x = dram_out("dbg_x", [128, KC, T])

    with ExitStack() as stack:
        sb = lambda n, s, d: stack.enter_context(nc.sbuf_tensor(n, s, d))
        S = Sched(nc, stack)
        PE, ACT, DVE, POOL, SP = S.pe, S.act, S.dve, S.pool, S.sp

        x = sb("x", [128, KC, T], F32)
        h = sb("h", [128, KC, T], BF16)
        ring = sb("ring", [128, NSLOT, 4096], BF16)
        scr = sb("scr", [128, NVS, 2048], BF16)
        scr_f = scr.bitcast(F32)
        actb = sb("actb", [128, 2, 4, 512], BF16)
        actb_f = actb.bitcast(F32).rearrange("p a b c -> p (a b c)")
        tmpf = sb("tmpf", [128, 3, 512], F32)
        priv = sb("priv", [128, 3408], F32)
        privb = priv.bitcast(BF16)
        lngb = priv[:, 0:1024].rearrange("p (u a b) -> p u a b", u=2, a=2)
        wmt = privb[:, 2048:4096].rearrange("p (g s) -> p g s", g=16)
        qbuf = privb[:, 4096:6144].rearrange("p (g s) -> p g s", g=16)
        dg = privb[:, 6144:6400].rearrange("p (g s) -> p g s", g=16)
        w00 = priv[:, 3200:3216]
        b0b = priv[:, 3216:3232]
        qs = priv[:, 3232:3248]
        stats = priv[:, 3248:3368].rearrange("p (c a b) -> p c a b", c=5, a=4)
        mv = priv[:, 3368:3408].rearrange("p (a b) -> p a b", a=5)
        state = priv[:, 0:512].rearrange("p (a b c) -> p a b c", a=KC, b=2)
        csz = priv[:, 512:768].rearrange("p (a b) -> p a b", a=KC)
        zs = priv[:, 768:800].rearrange("p (a b) -> p a b", a=2)
        cpst = priv[:, 800:832].rearrange("p (a b) -> p a b", a=KC)
        gains = sb("gains_s", [128, 9, KC], F32)
        lnfm = sb("lnfm_s", [128, 2, 2, KC], F32)
        convw = sb("convw_s", [128, 2, 3, KC], F32)
        onesm = sb("onesm", [128, 128], BF16)
        onesr0 = sb("onesr0", [128, 128], BF16)
        ident = sb("ident", [128, 128], F32)
        dummy = sb("dummyt", [128, 8], F32)
        ps = stack.enter_context(nc.psum_tensor("ps", [128, 8, 512], F32))

        Rx = [Res("x%d" % t) for t in range(3)]
        Rh = [Res("h%d" % t) for t in range(3)]
        Rslot = [Res("slot%d" % s) for s in range(NSLOT)]
        Rscr = [Res("scr%d" % s) for s in range(NVS)]
        S.guarded = {Rx[1], Rx[2], Rh[1], Rh[2]} | set(Rscr)
        Ract = [Res("act0"), Res("act1")]
        Rtmpf = [Res("tmpf%d" % i) for i in range(3)]
        Rlngb = [Res("lngb0"), Res("lngb1")]
        Rwmt = Res("wmt")
        Rconst = Res("const")
        Rgains = Res("gains")
        Rconvw = Res("convw")
        Rstate = Res("state")
        Rcp = Res("cpst")
        Rcsz = Res("csz")
        Rzs = Res("zs")
        Rq = Res("qbuf")
        Rqs = Res("qs")
        Rb0 = Res("b0")
        Rlnfm = Res("lnfm")
        Rw00 = Res("w00")
        Rdg = Res("dg")
        Rstats = [Res("stats%d" % c) for c in range(5)]
        Rmv = [Res("mv%d" % c) for c in range(5)]
        Rdummy = Res("dummy")
        Rps = [Res("ps%d" % b) for b in range(8)]
        priv_all = Rlngb + [Rstate, Rcp, Rcsz, Rzs]

        rot = {"up": 0, "dn": 0, "tmpf": 0}

        def nxt(kind, n, base=0):
            v = rot[kind]
            rot[kind] = (v + 1) % n
            return base + v

        up_bank = lambda: nxt("up", 4, 0)
        dn_bank = lambda: nxt("dn", 4, 4)
        tmp_slot = lambda: nxt("tmpf", 3)

        def priv_fence():
            S.op(POOL, lambda e: e.memset(dummy[:], 0.0), writes=priv_all + [Rdummy])

        loads = _load_sequence()
        ws = {"next": 0, "released": [False] * len(loads), "cursor": 0}

        def ws_pump():
            while ws["next"] < len(loads):
                i = ws["next"]
                if i >= NSLOT and not ws["released"][i - NSLOT]:
                    break
                s = i % NSLOT
                widx = WALL_IDX[loads[i]]
                S.dma(POOL, lambda e, s=s, widx=widx: e.dma_start(out=ring[:, s, :], in_=wall[widx]),
                      writes=[Rslot[s]], owner=Rslot[s])
                ws["next"] += 1

        def ws_take(key):
            i = ws["cursor"]
            assert loads[i] == key, (loads[i], key)
            assert i < ws["next"], "weight load %d not issued yet" % i
            ws["cursor"] += 1
            return i

        def ws_release(i):
            ws["released"][i] = True
            ws_pump()

        S.dma(SP, lambda e: e.dma_start(out=gains[:], in_=gains_d), writes=[Rgains])
        for t, (c0, c1) in enumerate(TILES):
            S.dma(SP, lambda e, c0=c0, c1=c1: e.dma_start(out=x[:, :, c0:c1], in_=xin[:, :, c0:c1]),
                  writes=[Rx[t]])
        S.dma(SP, lambda e: e.dma_start(out=convw[:], in_=convw_d), writes=[Rconvw])
        S.dma(SP, lambda e: e.dma_start(out=lnfm[:], in_=lnfm_d), writes=[Rlnfm])
        S.op(POOL, lambda e: e.memset(onesm[:], 1.0 / D), writes=[Rconst])
        S.op(POOL, lambda e: e.memset(ident[:], 1.0), writes=[Rconst])
        S.op(POOL, lambda e: e.affine_select(out=ident[:], in_=ident[:], pattern=[[-1, 128]],
                                             compare_op=ALU.is_equal, fill=0.0, base=0,
                                             channel_multiplier=1),
             reads=[Rconst], writes=[Rconst])
        S.op(POOL, lambda e: e.memset(onesr0[:], 0.0), writes=[Rconst])
        S.op(POOL, lambda e: e.memset(onesr0[0:1, :], 1.0), writes=[Rconst])
        S.op(POOL, lambda e: e.memset(priv[:], 0.0), writes=priv_all + [Rwmt, Rb0, Rw00, Rdg, Rq, Rqs] + Rstats + Rmv)
        S.op(POOL, lambda e: e.memset(scr[:, NVS - 1, :], 0.0), writes=[Rscr[NVS - 1]])

        early = {"done": False}

        def norm_sq(n, t):
            c0, c1 = TILES[t]
            N = c1 - c0
            for k in range(KC):
                dst = scr[:, k // 4, (k % 4) * 512:(k % 4) * 512 + N]
                src = x[:, k, c0:c1]
                if n == 0:
                    who = ("APDADAPDADADADAD" if t == 0 else "APAAPAAPAAPAAPAA")[k]
                else:
                    who = "AAAAAAAAAAAAPPPP"[k]
                if who == "P":
                    S.op(POOL, lambda e, dst=dst, src=src: e.tensor_tensor(out=dst, in0=src, in1=src, op=ALU.mult),
                         reads=[Rx[t]], writes=[Rscr[k // 4]])
                elif who == "D":
                    S.op(DVE, lambda e, dst=dst, src=src: e.tensor_tensor(out=dst, in0=src, in1=src, op=ALU.mult),
                         reads=[Rx[t]], writes=[Rscr[k // 4]])
                else:
                    S.op(ACT, lambda e, dst=dst, src=src: e.activation(out=dst, in_=src, func=AF.Square),
                         reads=[Rx[t]], writes=[Rscr[k // 4]])

        def early_sq(n):
            norm_sq(n, 0)
            early["done"] = True

        def norm_tile(n, t, final, skip_sq=False):
            c0, c1 = TILES[t]
            N = c1 - c0
            if not skip_sq:
                norm_sq(n, t)
            b = up_bank()
            fns = []
            for k in range(KC):
                rhs = scr[:, k // 4, (k % 4) * 512:(k % 4) * 512 + N]
                fns.append(lambda e, b=b, rhs=rhs, k=k, N=N: e.matmul(ps[:, b, 0:N], lhsT=onesm[:], rhs=rhs,
                                                                 start=(k == 0), stop=(k == KC - 1)))
            S.group(PE, fns, reads=Rscr[0:4] + [Rconst], writes=[Rps[b]])
            r = tmp_slot()
            S.op(ACT, lambda e, b=b, r=r, N=N: e.activation(out=tmpf[:, r, 0:N], in_=ps[:, b, 0:N], func=AF.Sqrt,
                                                       bias=RMS_EPS),
                 reads=[Rps[b]], writes=[Rtmpf[r]])
            S.op(DVE, lambda e, r=r, N=N: e.reciprocal(out=tmpf[:, r, 0:N], in_=tmpf[:, r, 0:N]),
                 reads=[Rtmpf[r]], writes=[Rtmpf[r]])
            fns = []
            for k in range(KC):
                dst = x[:, k, c0:c1] if final else h[:, k, c0:c1]
                fns.append(lambda e, dst=dst, k=k, r=r, N=N, c0=c0, c1=c1: e.scalar_tensor_tensor(
                    out=dst, in0=x[:, k, c0:c1], scalar=gains[:, n, k:k + 1], in1=tmpf[:, r, 0:N],
                    op0=ALU.mult, op1=ALU.mult))
            if final:
                S.group(DVE, fns, reads=[Rtmpf[r], Rgains, Rx[t]], writes=[Rx[t]])
                S.dma(SP, lambda e, c0=c0, c1=c1: e.dma_start(out=y_o[:, :, c0:c1], in_=x[:, :, c0:c1]),
                      reads=[Rx[t]], owner=Rx[t])
            else:
                S.group(DVE, fns, reads=[Rtmpf[r], Rgains, Rx[t]], writes=[Rh[t]])

        def rms_norm(n, final=False):
            S.flush()
            skip = early["done"]
            early["done"] = False
            norm_tile(n, 0, final, skip_sq=skip)
            if n == 0:
                ws_pump()
            for t in (1, 2):
                S.pending.append(lambda t=t: norm_tile(n, t, final))
            if final:
                S.flush()

        def tiles_of(c0, c1):
            return [t for t, (a, b) in enumerate(TILES) if a < c1 and c0 < b]

        def down_unit(t, par, nk, slot_of, off_of, cols=None):
            c0, c1 = TILES[t] if cols is None else cols
            rxs = [Rx[q] for q in tiles_of(c0, c1)]
            N = c1 - c0
            for fo in range(KC):
                b = dn_bank()
                fns = []
                rslots = set()
                for kf in range(nk):
                    s = slot_of(kf)
                    rslots.add(s)
                    o = off_of(kf) + fo * 128
                    fns.append(lambda e, b=b, s=s, o=o, kf=kf, N=N: e.matmul(
                        ps[:, b, 0:N], lhsT=ring[:, s, o:o + 128], rhs=actb[:, par, kf, 0:N],
                        start=(kf == 0), stop=(kf == nk - 1)))
                S.group(PE, fns, reads=[Ract[par]] + [Rslot[s] for s in sorted(rslots)], writes=[Rps[b]])
                S.op(DVE, lambda e, b=b, fo=fo, N=N, c0=c0, c1=c1: e.tensor_tensor(
                    out=x[:, fo, c0:c1], in0=ps[:, b, 0:N], in1=x[:, fo, c0:c1], op=ALU.add),
                     reads=[Rps[b]], writes=rxs)

        def run_units(units, up_fn, down_fn, hook_after=None, hook=None):
            def dn(i):
                down_fn(units[i], i % 2)
                if hook is not None and units[i] == hook_after:
                    hook()
            for i, u in enumerate(units):
                up_fn(u, i % 2)
                if i > 0:
                    dn(i - 1)
            dn(len(units) - 1)

        def ffn(i):
            blk = {}

            def up_fn(u, par):
                b, t = u
                if t == 0:
                    blk[b] = [ws_take(("up", i, 2 * b)), ws_take(("up", i, 2 * b + 1)), None, None]
                c0, c1 = TILES[t]
                N = c1 - c0
                for f in range(4):
                    s = blk[b][f // 2] % NSLOT
                    bk = up_bank()
                    fns = []
                    for k in range(KC):
                        o = k * 256 + (f % 2) * 128
                        fns.append(lambda e, bk=bk, s=s, o=o, k=k, N=N, c0=c0, c1=c1: e.matmul(
                            ps[:, bk, 0:N], lhsT=ring[:, s, o:o + 128], rhs=h[:, k, c0:c1],
                            start=(k == 0), stop=(k == KC - 1)))
                    S.group(PE, fns, reads=[Rh[t], Rslot[s]], writes=[Rps[bk]])
                    S.op(ACT, lambda e, bk=bk, N=N: e.activation(out=ps[:, bk, 0:N], in_=ps[:, bk, 0:N], func=AF.Relu),
                         reads=[Rps[bk]], writes=[Rps[bk]])
                    S.op(ACT, lambda e, bk=bk, f=f, N=N, par=par: e.activation(out=actb[:, par, f, 0:N], in_=ps[:, bk, 0:N],
                                                                          func=AF.Square),
                         reads=[Rps[bk]], writes=[Ract[par]])
                if t == len(TILES) - 1:
                    ws_release(blk[b][0])
                    ws_release(blk[b][1])

            def down_fn(u, par):
                b, t = u
                if t == 0:
                    blk[b][2] = ws_take(("down", i, 2 * b))
                    blk[b][3] = ws_take(("down", i, 2 * b + 1))
                down_unit(t, par, 4, lambda kf: blk[b][2 + kf // 2] % NSLOT, lambda kf: (kf % 2) * 2048)
                if t == len(TILES) - 1:
                    ws_release(blk[b][2])
                    ws_release(blk[b][3])

            units = [(b, t) for b in range(16) for t in range(len(TILES))]
            run_units(units, up_fn, down_fn, hook_after=(15, 0), hook=lambda: early_sq(2 * i + 2))

        def a_setup(j):
            priv_fence()
            for q in range(2):
                S.dma(SP, lambda e, q=q: e.dma_start(
                    out=scr_f[:, q, :].rearrange("p (g s) -> p g s", g=8),
                    in_=aws_d[j, 8 * q:8 * q + 8].rearrange("g t s -> t g s")),
                      writes=[Rscr[q]])
            for g in range(16):
                q = g // 8
                sl = scr_f[:, q, (g % 8) * 128:(g % 8) * 128 + 128]
                S.op(POOL, lambda e, sl=sl: e.affine_select(out=sl, in_=sl, pattern=[[-1, 128]],
                                                            compare_op=ALU.is_ge, fill=0.0, base=0,
                                                            channel_multiplier=1),
                     reads=[Rscr[q]], writes=[Rscr[q]])
            for qq in range(4):
                bk = up_bank()
                fns = []
                for gl in range(4):
                    g = 4 * qq + gl
                    sl = scr_f[:, g // 8, (g % 8) * 128:(g % 8) * 128 + 128]
                    fns.append(lambda e, bk=bk, gl=gl, sl=sl: e.transpose(ps[:, bk, gl * 128:gl * 128 + 128], sl, ident[:]))
                S.group(PE, fns, reads=[Rscr[0], Rscr[1], Rconst], writes=[Rps[bk]])
                S.op(ACT, lambda e, bk=bk, qq=qq: e.activation(
                    out=wmt[:, 4 * qq:4 * qq + 4, :].rearrange("p g s -> p (g s)"), in_=ps[:, bk, :], func=AF.Copy),
                     reads=[Rps[bk]], writes=[Rwmt])
            S.dma(SP, lambda e: e.dma_start(out=w00[:, :], in_=w00_d[j].partition_broadcast(128)), writes=[Rw00])
            S.dma(SP, lambda e: e.dma_start(out=b0b[:, :], in_=b0_d[j].partition_broadcast(128)), writes=[Rb0])
            for g in range(16):
                S.op(DVE, lambda e, g=g: e.tensor_scalar(out=dg[:, g, :], in0=ident[:, 0:NSMP],
                                                        scalar1=w00[:, g:g + 1], scalar2=None, op0=ALU.mult),
                     reads=[Rw00, Rconst], writes=[Rdg])
            S.op(DVE, lambda e: e.tensor_tensor(out=qs[:, :], in0=lnfm[:, j, 1, :], in1=w00[:, :], op=ALU.mult),
                 reads=[Rlnfm, Rw00], writes=[Rqs])
            S.op(DVE, lambda e: e.tensor_tensor(out=qs[:, :], in0=qs[:, :], in1=b0b[:, :], op=ALU.add),
                 reads=[Rqs, Rb0], writes=[Rqs])
            S.op(POOL, lambda e: e.memset(scr[:, 3, :], 0.0), writes=[Rscr[3]])
            S.dma(POOL, lambda e: e.dma_start(out=scr[0:1, 3, :], in_=abs_d[j:j + 1, :]), writes=[Rscr[3]])
            S.dma(POOL, lambda e: e.dma_start(out=scr[:, 4, :], in_=lnb_d[j].partition_broadcast(128)), writes=[Rscr[4]])
            for qq in range(4):
                bk = up_bank()
                fns = []
                for gl in range(4):
                    g = 4 * qq + gl
                    fns.append(lambda e, bk=bk, gl=gl, g=g: e.matmul(
                        ps[:, bk, gl * 128:(gl + 1) * 128], lhsT=scr[:, 4, g * 128:(g + 1) * 128], rhs=wmt[:, g, :],
                        start=(gl == 0), stop=False, skip_group_check=True))
                    fns.append(lambda e, bk=bk, gl=gl, g=g: e.matmul(
                        ps[:, bk, gl * 128:(gl + 1) * 128], lhsT=onesr0[:], rhs=scr[:, 3, g * 128:(g + 1) * 128],
                        start=False, stop=(gl == 3), skip_group_check=True))
                S.group(PE, fns, reads=[Rscr[3], Rscr[4], Rwmt, Rconst], writes=[Rps[bk]])
                S.op(ACT, lambda e, bk=bk, qq=qq: e.activation(
                    out=qbuf[:, 4 * qq:4 * qq + 4, :].rearrange("p g s -> p (g s)"), in_=ps[:, bk, :], func=AF.Copy),
                     reads=[Rps[bk]], writes=[Rq])

        def a_mixer(i, j):
            a_setup(j)
            nbrow = [0]
            lnrot = [0]

            def a_pass(ptiles):
                pc0, pc1 = ptiles[0][0], ptiles[-1][1]
                chunks = []
                cc = pc0
                while cc < min(pc1, TP):
                    chunks.append((len(chunks), cc, 128))
                    cc += 128
                if pc1 > TP:
                    chunks.append((len(chunks), TP, NSMP))
                assert len(chunks) <= NVS
                rh_of = lambda cc, M: [Rh[q] for q in tiles_of(cc, cc + M)]
                for n in range(8):
                    li = ws_take(("a_in", j, 8 + n))
                    s = li % NSLOT
                    for (vs, cc, M) in chunks:
                        bk = up_bank()
                        fns = []
                        for k in range(KC):
                            fns.append(lambda e, bk=bk, s=s, k=k, cc=cc, M=M: e.matmul(
                                ps[0:M, bk, 0:256], lhsT=h[:, k, cc:cc + M], rhs=ring[:, s, k * 256:(k + 1) * 256],
                                start=(k == 0), stop=(k == KC - 1)))
                        S.group(PE, fns, reads=rh_of(cc, M) + [Rslot[s]], writes=[Rps[bk]])
                        S.op(ACT, lambda e, bk=bk, vs=vs, M=M, n=n: e.activation(
                            out=scr[0:M, vs, n * 256:(n + 1) * 256], in_=ps[0:M, bk, 0:256], func=AF.Gelu),
                             reads=[Rps[bk]], writes=[Rscr[vs]])
                    ws_release(li)
                for (vs, cc, M) in chunks:
                    fns = []
                    for q in range(4):
                        fns.append(lambda e, vs=vs, M=M, q=q: e.bn_stats(out=stats[0:M, vs, q, :],
                                                                       in_=scr[0:M, vs, q * 512:(q + 1) * 512]))
                    S.group(DVE, fns, reads=[Rscr[vs]], writes=[Rstats[vs]])
                    S.op(DVE, lambda e, M=M, vs=vs: e.bn_aggr(out=mv[0:M, vs, 0:2],
                                                           in_=stats[0:M, vs, :, :].rearrange("p a b -> p (a b)")),
                         reads=[Rstats[vs]], writes=[Rmv[vs]])
                for (vs, cc, M) in chunks:
                    S.op(ACT, lambda e, M=M, vs=vs: e.activation(out=mv[0:M, vs, 2:3], in_=mv[0:M, vs, 1:2], func=AF.Sqrt,
                                                              bias=LN_EPS),
                         reads=[Rmv[vs]], writes=[Rmv[vs]])
                for (vs, cc, M) in chunks:
                    S.op(DVE, lambda e, M=M, vs=vs: e.reciprocal(out=mv[0:M, vs, 3:4], in_=mv[0:M, vs, 2:3]),
                         reads=[Rmv[vs]], writes=[Rmv[vs]])
                    S.op(DVE, lambda e, M=M, vs=vs: e.tensor_scalar(out=mv[0:M, vs, 4:5], in0=mv[0:M, vs, 0:1], scalar1=-1.0,
                                                                 scalar2=mv[0:M, vs, 3:4], op0=ALU.mult, op1=ALU.mult),
                         reads=[Rmv[vs]], writes=[Rmv[vs]])
                for (vs, cc, M) in chunks:
                    S.op(ACT, lambda e, vs=vs, M=M: e.activation(
                        out=scr[0:M, vs, :], in_=scr[0:M, vs, :], func=AF.Identity,
                        scale=mv[0:M, vs, 3:4], bias=mv[0:M, vs, 4:5]),
                         reads=[Rscr[vs], Rmv[vs]], writes=[Rscr[vs]])
                tails = [c for c in chunks if (c[2] == NSMP) or (c[1] == TP - 128)]
                tpiece = [0]

                def ln_piece_load(q):
                    lb = q % 2
                    S.dma(SP, lambda e, q=q, lb=lb: e.dma_start(out=lngb[:, lb], in_=lngb_d[j, q].partition_broadcast(128)),
                          writes=[Rlngb[lb]])

                def tail_piece():
                    q = tpiece[0]
                    if not tails or q >= 8:
                        return
                    tpiece[0] += 1
                    lb = q % 2
                    if q + 1 < 8:
                        ln_piece_load(q + 1)
                    cs_ = slice(q * 256, (q + 1) * 256)
                    for (vs, cc, M) in tails:
                        tf = tmp_slot()
                        S.op(DVE, lambda e, vs=vs, M=M, tf=tf, lb=lb, cs_=cs_: e.tensor_tensor(
                            out=tmpf[0:M, tf, 0:256], in0=scr[0:M, vs, cs_], in1=lngb[0:M, lb, 0, :], op=ALU.mult),
                             reads=[Rscr[vs], Rlngb[lb]], writes=[Rtmpf[tf]])
                        S.op(DVE, lambda e, M=M, tf=tf, lb=lb: e.tensor_tensor(
                            out=tmpf[0:M, tf, 0:256], in0=tmpf[0:M, tf, 0:256], in1=lngb[0:M, lb, 1, :], op=ALU.add),
                             reads=[Rtmpf[tf], Rlngb[lb]], writes=[Rtmpf[tf]])
                        dst = vs_o if M == NSMP else v_o
                        S.dma(SP, lambda e, dst=dst, M=M, cs_=cs_, tf=tf: e.dma_start(out=dst[j, :, cs_], in_=tmpf[0:M, tf, 0:256]),
                              reads=[Rtmpf[tf]], owner=Rtmpf[tf])

                if tails:
                    ln_piece_load(0)
                blk = {}
                nt = len(ptiles)

                def up_fn(u, par):
                    gb, ti = u
                    c0, c1 = ptiles[ti]
                    N = c1 - c0
                    if ti == 0:
                        blk[gb] = [ws_take(("a_in", j, 2 * gb)), ws_take(("a_in", j, 2 * gb + 1)), None, None]
                    tch = [c for c in chunks if c0 <= c[1] < c1]
                    npc = len([c for c in tch if c[2] == 128])
                    npr = npc * 128
                    rhs_h = [Rh[q] for q in tiles_of(c0, c1)]
                    for gl in range(4):
                        g = gb * 4 + gl
                        s = blk[gb][gl // 2] % NSLOT
                        bu = up_bank()
                        fns = []
                        for k in range(KC):
                            o = k * 256 + (gl % 2) * 128
                            fns.append(lambda e, bu=bu, s=s, o=o, k=k: e.matmul(
                                ps[:, bu, 0:N], lhsT=ring[:, s, o:o + 128], rhs=h[:, k, c0:c1],
                                start=(k == 0), stop=(k == KC - 1)))
                        S.group(PE, fns, reads=rhs_h + [Rslot[s]], writes=[Rps[bu]])
                        bf = up_bank()
                        fns = []
                        for ci_, (vs, cc, M) in enumerate(tch):
                            po = cc - c0
                            first = (ci_ == 0)
                            last = (ci_ == len(tch) - 1)
                            if M == 128:
                                fns.append(lambda e, bf=bf, vs=vs, g=g, po=po, first=first, last=last: e.matmul(
                                    ps[:, bf, po:po + 128], lhsT=scr[:, vs, g * 128:(g + 1) * 128], rhs=wmt[:, g, :],
                                    start=first, stop=last, skip_group_check=True))
                            else:
                                fns.append(lambda e, bf=bf, vs=vs, g=g, po=po, first=first, last=last: e.matmul(
                                    ps[:, bf, po:po + NSMP], lhsT=scr[:, vs, g * 128:(g + 1) * 128], rhs=dg[:, g, :],
                                    start=first, stop=last, skip_group_check=True))
                        S.group(PE, fns, reads=[Rscr[c[0]] for c in tch] + [Rwmt, Rdg], writes=[Rps[bf]])
                        tf = tmp_slot()
                        S.op(ACT, lambda e, bu=bu, tf=tf: e.activation(out=tmpf[:, tf, 0:N], in_=ps[:, bu, 0:N], func=AF.Gelu),
                             reads=[Rps[bu]], writes=[Rtmpf[tf]])
                        if npc > 0:
                            S.op(DVE, lambda e, bf=bf, g=g, npc=npc, npr=npr: e.scalar_tensor_tensor(
                                out=ps[:, bf, 0:npr].rearrange("p (n t) -> p n t", n=npc),
                                in0=ps[:, bf, 0:npr].rearrange("p (n t) -> p n t", n=npc),
                                scalar=lnfm[:, j, 0, g:g + 1],
                                in1=qbuf[:, g, :].unsqueeze(1).broadcast_to([128, npc, 128]),
                                op0=ALU.mult, op1=ALU.add),
                                 reads=[Rps[bf], Rq, Rlnfm], writes=[Rps[bf]])
                        if N > npr:
                            S.op(DVE, lambda e, bf=bf, g=g, npr=npr: e.tensor_scalar(
                                out=ps[:, bf, npr:N], in0=ps[:, bf, npr:N], scalar1=lnfm[:, j, 0, g:g + 1],
                                scalar2=qs[:, g:g + 1], op0=ALU.mult, op1=ALU.add),
                                 reads=[Rps[bf], Rqs, Rlnfm], writes=[Rps[bf]])
                        S.op(DVE, lambda e, bf=bf, tf=tf, gl=gl, par=par: e.tensor_tensor(
                            out=actb[:, par, gl, 0:N], in0=ps[:, bf, 0:N], in1=tmpf[:, tf, 0:N], op=ALU.mult),
                             reads=[Rps[bf], Rtmpf[tf]], writes=[Ract[par]])
                    if ti == nt - 1:
                        ws_release(blk[gb][0])
                        ws_release(blk[gb][1])
                    tail_piece()

                def down_fn(u, par):
                    gb, ti = u
                    if ti == 0:
                        blk[gb][2] = ws_take(("a_out", j, 2 * gb))
                        blk[gb][3] = ws_take(("a_out", j, 2 * gb + 1))
                    down_unit(None, par, 4, lambda kf: blk[gb][2 + kf // 2] % NSLOT, lambda kf: (kf % 2) * 2048,
                              cols=ptiles[ti])
                    if ti == nt - 1:
                        ws_release(blk[gb][2])
                        ws_release(blk[gb][3])

                units = [(gb, ti) for gb in range(4) for ti in range(nt)]
                run_units(units, up_fn, down_fn)
                while tails and tpiece[0] < 8:
                    tail_piece()

            for ptiles in A_PASSES:
                a_pass(ptiles)

        def b_mixer(i, j):
            priv_fence()
            zflat = lambda fl: scr_f[:, 2 * fl:2 * fl + 2, :].rearrange("p a b -> p (a b)")
            S.dma(SP, lambda e: e.dma_start(out=state, in_=state_d[:, j]), writes=[Rstate])
            for fl in range(2):
                S.op(POOL, lambda e, fl=fl: e.memset(zflat(fl)[:, 0:2], 0.0), writes=[Rscr[2 * fl], Rscr[2 * fl + 1]])
            blk = {}

            def up_fn(u, par):
                bb, t = u
                if t == 0:
                    blk[bb] = [ws_take(("b_in", j, bb)), ws_take(("b_in", j, 8 + bb)), ws_take(("b_in", j, 16 + bb)), None]
                c0, c1 = TILES[t]
                N = c1 - c0
                npr = min(c1, TP) - c0
                for fl in range(2):
                    fc = 2 * bb + fl
                    banks = [None, None, None]
                    for part in (1, 2, 0):
                        s = blk[bb][part] % NSLOT
                        bk = up_bank()
                        banks[part] = bk
                        fns = []
                        for k in range(KC):
                            o = k * 256 + fl * 128
                            fns.append(lambda e, bk=bk, s=s, o=o, k=k: e.matmul(
                                ps[:, bk, 0:N], lhsT=ring[:, s, o:o + 128], rhs=h[:, k, c0:c1],
                                start=(k == 0), stop=(k == KC - 1)))
                        S.group(PE, fns, reads=[Rh[t], Rslot[s]], writes=[Rps[bk]])
                    pB, pC, pH = banks
                    tf = tmp_slot()
                    S.op(ACT, lambda e, pC=pC, tf=tf: e.activation(out=tmpf[:, tf, 0:N], in_=ps[:, pC, 0:N], func=AF.Copy),
                         reads=[Rps[pC]], writes=[Rtmpf[tf]])
                    zf = zflat(fl)
                    Rz = [Rscr[2 * fl], Rscr[2 * fl + 1]]
                    w = lambda kk, fc=fc: convw[:, j, kk, fc:fc + 1]
                    S.op(DVE, lambda e, pH=pH, tf=tf, zf=zf: e.tensor_tensor(
                        out=zf[:, 2 + c0:2 + c0 + npr], in0=ps[:, pH, 0:npr], in1=tmpf[:, tf, 0:npr], op=ALU.mult),
                         reads=[Rps[pH], Rtmpf[tf]], writes=Rz)
                    if N > npr:
                        S.op(DVE, lambda e, pH=pH, tf=tf, fl=fl: e.tensor_tensor(
                            out=zs[:, fl, :], in0=ps[:, pH, npr:N], in1=tmpf[:, tf, npr:N], op=ALU.mult),
                             reads=[Rps[pH], Rtmpf[tf]], writes=[Rzs])
                    S.op(DVE, lambda e, zf=zf, tf=tf, w=w: e.tensor_scalar(
                        out=tmpf[:, tf, 0:npr], in0=zf[:, 2 + c0:2 + c0 + npr], scalar1=w(2), scalar2=None, op0=ALU.mult),
                         reads=Rz + [Rconvw], writes=[Rtmpf[tf]])
                    S.op(DVE, lambda e, zf=zf, tf=tf, w=w: e.scalar_tensor_tensor(
                        out=tmpf[:, tf, 0:npr], in0=zf[:, 1 + c0:1 + c0 + npr], scalar=w(1), in1=tmpf[:, tf, 0:npr],
                        op0=ALU.mult, op1=ALU.add),
                         reads=Rz + [Rconvw, Rtmpf[tf]], writes=[Rtmpf[tf]])
                    S.op(DVE, lambda e, zf=zf, tf=tf, w=w: e.scalar_tensor_tensor(
                        out=tmpf[:, tf, 0:npr], in0=zf[:, c0:c0 + npr], scalar=w(0), in1=tmpf[:, tf, 0:npr],
                        op0=ALU.mult, op1=ALU.add),
                         reads=Rz + [Rconvw, Rtmpf[tf]], writes=[Rtmpf[tf]])
                    if N > npr:
                        S.op(DVE, lambda e, fl=fl, tf=tf, w=w: e.tensor_scalar(
                            out=tmpf[:, tf, npr:N], in0=zs[:, fl, :], scalar1=w(2), scalar2=None, op0=ALU.mult),
                             reads=[Rzs, Rconvw], writes=[Rtmpf[tf]])
                        S.op(DVE, lambda e, tf=tf, fc=fc, w=w: e.scalar_tensor_tensor(
                            out=tmpf[:, tf, npr:N], in0=state[:, fc, 1, :], scalar=w(1), in1=tmpf[:, tf, npr:N],
                            op0=ALU.mult, op1=ALU.add),
                             reads=[Rstate, Rconvw, Rtmpf[tf]], writes=[Rtmpf[tf]])
                        S.op(DVE, lambda e, tf=tf, fc=fc, w=w: e.scalar_tensor_tensor(
                            out=tmpf[:, tf, npr:N], in0=state[:, fc, 0, :], scalar=w(0), in1=tmpf[:, tf, npr:N],
                            op0=ALU.mult, op1=ALU.add),
                             reads=[Rstate, Rconvw, Rtmpf[tf]], writes=[Rtmpf[tf]])
                        S.op(POOL, lambda e, fl=fl, fc=fc: e.tensor_copy(out=csz[:, fc, :], in_=zs[:, fl, :]),
                             reads=[Rzs], writes=[Rcsz])
                        S.op(POOL, lambda e, zf=zf, fc=fc: e.tensor_copy(out=cpst[:, fc, :], in_=zf[:, TP:TP + 2]),
                             reads=Rz, writes=[Rcp])
                    S.op(DVE, lambda e, pB=pB, tf=tf, fl=fl, par=par: e.tensor_tensor(
                        out=actb[:, par, fl, 0:N], in0=ps[:, pB, 0:N], in1=tmpf[:, tf, 0:N], op=ALU.mult),
                         reads=[Rps[pB], Rtmpf[tf]], writes=[Ract[par]])
                if t == len(TILES) - 1:
                    for q in range(3):
                        ws_release(blk[bb][q])

            def down_fn(u, par):
                bb, t = u
                if t == 0:
                    blk[bb][3] = ws_take(("b_out", j, bb))
                down_unit(t, par, 2, lambda kf: blk[bb][3] % NSLOT, lambda kf: kf * 2048)
                if t == len(TILES) - 1:
                    ws_release(blk[bb][3])

            units = [(bb, t) for bb in range(8) for t in range(len(TILES))]
            run_units(units, up_fn, down_fn)
            S.dma(SP, lambda e: e.dma_start(out=cp_o[:, j], in_=cpst), reads=[Rcp], owner=Rcp)
            S.dma(SP, lambda e: e.dma_start(out=cs_o[:, j, :, 1, :], in_=csz), reads=[Rcsz], owner=Rcsz)
            S.dma(SP, lambda e: e.dma_start(out=cs_o[:, j, :, 0, :], in_=state[:, :, 1, :]), reads=[Rstate], owner=Rstate)

        for i in range(n_layers):
            j = i // 2
            rms_norm(2 * i)
            if debug and i == 0:
                for t_ in range(3):
                    c0_, c1_ = TILES[t_]
                    S.dma(POOL, lambda e, c0_=c0_, c1_=c1_: e.dma_start(out=dbg_h[:, :, c0_:c1_], in_=h[:, :, c0_:c1_]),
                          reads=[Rh[t_]], owner=Rh[t_])
            if i % 2 == 0:
                a_mixer(i, j)
            else:
                b_mixer(i, j)
            if debug and i == 0:
                for t_ in range(3):
                    c0_, c1_ = TILES[t_]
                    S.dma(SP, lambda e, c0_=c0_, c1_=c1_: e.dma_start(out=dbg_x[:, :, c0_:c1_], in_=x[:, :, c0_:c1_]),
                          reads=[Rx[t_]], owner=Rx[t_])
            rms_norm(2 * i + 1)
            ffn(i)
        rms_norm(8, final=True)
        S.flush()
        for R in S.all_dma_res:
            SP.ops.append(lambda e, R=R: e.wait_ge(R.dsem, R.dcnt))

        stat = {k: len(v.ops) for k, v in (("pe", PE), ("act", ACT), ("dve", DVE), ("pool", POOL), ("sp", SP))}
        print("[kernel] ops per engine:", stat, "loads", ws["next"], "/", len(loads), flush=True)
        with nc.Block() as block:
            @block.tensor
            def _(e):
                for f in PE.ops:
                    f(e)

            @block.scalar
            def _(e):
                for f in ACT.ops:
                    f(e)

            @block.vector
            def _(e):
                for f in DVE.ops:
                    f(e)

            @block.gpsimd
            def _(e):
                for f in POOL.ops:
                    f(e)

            @block.sync
            def _(e):
                for f in SP.ops:
                    f(e)
    return nc


def _cblocks(W):
    K, N = W.shape
    assert K == D
    return W.reshape(KC, 128, N // 256, 256).transpose(2, 1, 0, 3).reshape(N // 256, 128, 4096)


def _rblocks(W):
    K, N = W.shape
    assert N == D
    return W.reshape(K // 256, 2, 128, D).transpose(0, 2, 1, 3).reshape(K // 256, 128, 4096)


def _build_wall(a_w_in, a_w_out, b_w_in, b_w_out, ffn_w_up, ffn_w_down):
    wall = np.empty((NWALL, 128, 4096), dtype=np.float32)
    for j in range(2):
        s = WALL_IDX[("a_in", j, 0)]; wall[s:s + 16] = _cblocks(a_w_in[j])
        s = WALL_IDX[("a_out", j, 0)]; wall[s:s + 8] = _rblocks(a_w_out[j])
        s = WALL_IDX[("b_in", j, 0)]; wall[s:s + 24] = _cblocks(b_w_in[j])
        s = WALL_IDX[("b_out", j, 0)]; wall[s:s + 8] = _rblocks(b_w_out[j])
    for i in range(DEPTH):
        s = WALL_IDX[("up", i, 0)]; wall[s:s + 32] = _cblocks(ffn_w_up[i])
        s = WALL_IDX[("down", i, 0)]; wall[s:s + 32] = _rblocks(ffn_w_down[i])
    return wall


_NC_CACHE = {}


def _prepare(x_prompt, x_sample, state_conv, norm_mix_g, norm_ffn_g, a_w_in, a_ln_g, a_ln_b, a_w_s,
             a_b_s, a_w_out, b_w_in, b_conv_w, b_w_out, ffn_w_up, ffn_w_down, final_norm_g, cores=range(8)):
    f = lambda a: np.ascontiguousarray(np.asarray(a, dtype=np.float32))
    x_prompt, x_sample, state_conv = f(x_prompt), f(x_sample), f(state_conv)
    norm_mix_g, norm_ffn_g, final_norm_g = f(norm_mix_g), f(norm_ffn_g), f(final_norm_g)
    a_w_in, a_ln_g, a_ln_b, a_w_s, a_b_s, a_w_out = f(a_w_in), f(a_ln_g), f(a_ln_b), f(a_w_s), f(a_b_s), f(a_w_out)
    b_w_in, b_conv_w, b_w_out, ffn_w_up, ffn_w_down = f(b_w_in), f(b_conv_w), f(b_w_out), f(ffn_w_up), f(ffn_w_down)

    wall = _build_wall(a_w_in, a_w_out, b_w_in, b_w_out, ffn_w_up, ffn_w_down)
    gl = []
    for i in range(DEPTH):
        gl.append(norm_mix_g[i]); gl.append(norm_ffn_g[i])
    gl.append(final_norm_g)
    gains = np.ascontiguousarray(np.stack(gl, 0).reshape(9, KC, 128).transpose(2, 0, 1))
    convw = np.ascontiguousarray(b_conv_w.reshape(2, 3, KC, 128).transpose(3, 0, 1, 2))
    lngb = np.ascontiguousarray(np.stack([a_ln_g.reshape(2, 8, 256), a_ln_b.reshape(2, 8, 256)], axis=2))
    abs_ = np.ascontiguousarray(a_b_s.reshape(2, D))
    w00 = np.ascontiguousarray(a_w_s[:, :, 0, 0])
    b0 = np.ascontiguousarray(a_b_s[:, :, 0])
    lnfm = np.ascontiguousarray(np.stack([a_ln_g.reshape(2, KC, 128), a_ln_b.reshape(2, KC, 128)], axis=1).transpose(3, 0, 1, 2))

    in_maps = []
    for c in cores:
        b, half = c // 2, c % 2
        t0 = 0 if half == 0 else 2048 - TP
        xt = np.concatenate([x_prompt[b, t0:t0 + TP, :], x_sample[NSMP * c:NSMP * (c + 1), 0, :]], axis=0)
        xin = np.ascontiguousarray(xt.reshape(T, KC, 128).transpose(2, 1, 0))
        st = state_conv[:, NSMP * c:NSMP * (c + 1)]
        st = np.ascontiguousarray(st.reshape(2, NSMP, 2, KC, 128).transpose(4, 0, 3, 2, 1))
        in_maps.append({"xin": xin, "wall": wall, "gains": gains, "convw": convw, "state": st,
                        "lngb": lngb, "aws": a_w_s, "abs": abs_, "w00": w00, "b0": b0, "lnfm": lnfm, "lnb": a_ln_b})
    return in_maps


def _assemble(R):
    y_prompt = np.empty((4, 2048, D), np.float32)
    y_sample = np.empty((128, 1, D), np.float32)
    v_prompt = np.empty((2, 4, 128, D), np.float32)
    v_sample = np.empty((2, 128, 1, D), np.float32)
    conv_prompt = np.empty((2, 4, 2, D), np.float32)
    conv_sample = np.empty((2, 128, 2, D), np.float32)
    for c in range(8):
        b, half = c // 2, c % 2
        r = R[c]
        yt = np.asarray(r["y_fm"]).transpose(2, 1, 0).reshape(T, D)
        if half == 0:
            y_prompt[b, 0:TP] = yt[0:TP]
        else:
            y_prompt[b, TP:2048] = yt[2 * TP - 2048:TP]
        y_sample[NSMP * c:NSMP * (c + 1), 0] = yt[TP:T]
        v_sample[:, NSMP * c:NSMP * (c + 1), 0] = np.asarray(r["vs_out"])
        cs = np.asarray(r["convs"])
        conv_sample[:, NSMP * c:NSMP * (c + 1)] = cs.transpose(1, 4, 3, 2, 0).reshape(2, NSMP, 2, D)
        if half == 1:
            v_prompt[:, b] = np.asarray(r["v_out"])
            cp = np.asarray(r["convp"])
            conv_prompt[:, b] = cp.transpose(1, 3, 2, 0).reshape(2, 2, D)
    return (y_prompt, y_sample, v_prompt, v_sample, conv_prompt, conv_sample)


def kernel(**inputs):
    in_maps = _prepare(**inputs)
    if "nc" not in _NC_CACHE:
        _NC_CACHE["nc"] = build_program()
    nc = _NC_CACHE["nc"]
    res = run_bass_kernel_spmd(nc, in_maps, core_ids=list(range(8)))
    return _assemble(res.results)
```
